# Optimizing a Trainium2 kernel written in Bass

```python
import math
import jax, jax.numpy as jnp
from jax import lax
import numpy as np

D_MODEL = 1024
BATCH = 8
SEQ = 8192
DEPTH = 2

GRID_W = 64
CTX_LEN = 256
N_MIXERS = 2
N_HEADS = 16
N_KV_HEADS = 4
HEAD_DIM = D_MODEL // N_HEADS
GROUP = N_HEADS // N_KV_HEADS
WINDOW = 128
BLOCK = 128
ROPE_THETA = 10000.0
ROPE_HALF = HEAD_DIM // 2
AXIS_FREQS = ROPE_HALF // 2
ATTN_SCALE = HEAD_DIM ** -0.5
D_FF = ((-(-8 * D_MODEL // 3) + 255) // 256) * 256
HY_EMB = 33
HY_BANDS = (HY_EMB - 1) // 2
HY_HIDDEN = 64
HY_SHORT = 3
HY_DECAY_TARGET = 1e-2
HY_FAST_PCT = 0.3
HY_SLOW_PCT = 1.5
EPS = 1e-6
NEG = -1e30

kernel_name = "hybrid_swa_hyena_dit_block"


def rms_norm(x, g):
    xf = x.astype(jnp.float32)
    y = xf * lax.rsqrt(jnp.mean(xf * xf, axis=-1, keepdims=True) + EPS)
    return (y * g.astype(jnp.float32)).astype(x.dtype)


def modulate(h, shift, scale):
    return h * (1 + scale) + shift


def ada_chunks(cond, w, b):
    m = jax.nn.silu(cond) @ w + b
    return jnp.split(m[:, None, :], 6, axis=-1)


def axial_rope_tables(L):
    rows = L // GRID_W
    row = jnp.repeat(jnp.arange(rows, dtype=jnp.float32), GRID_W)
    col = jnp.tile(jnp.arange(GRID_W, dtype=jnp.float32), rows)
    inv_freq = ROPE_THETA ** (-jnp.arange(AXIS_FREQS, dtype=jnp.float32) / AXIS_FREQS)
    ang = jnp.concatenate([row[:, None] * inv_freq, col[:, None] * inv_freq], axis=-1)
    return jnp.cos(ang), jnp.sin(ang)


def apply_rope(x, cos, sin):
    c = cos[None, :, None, :].astype(x.dtype)
    s = sin[None, :, None, :].astype(x.dtype)
    x1, x2 = x[..., :ROPE_HALF], x[..., ROPE_HALF:]
    return jnp.concatenate([x1 * c - x2 * s, x1 * s + x2 * c], axis=-1)


def q_proj(h, wqkv, q_gain):
    B, L, _ = h.shape
    q = (h @ wqkv[:, :N_HEADS * HEAD_DIM]).reshape(B, L, N_HEADS, HEAD_DIM)
    return rms_norm(q, q_gain)


def kv_proj(h, wqkv, k_gain):
    B, L, _ = h.shape
    kv = h @ wqkv[:, N_HEADS * HEAD_DIM:]
    k = kv[..., :N_KV_HEADS * HEAD_DIM].reshape(B, L, N_KV_HEADS, HEAD_DIM)
    v = kv[..., N_KV_HEADS * HEAD_DIM:].reshape(B, L, N_KV_HEADS, HEAD_DIM)
    return rms_norm(k, k_gain), v


def sink_softmax(scores, values, sink):
    sink = sink.astype(jnp.float32).reshape(N_KV_HEADS, GROUP)[None, :, :, None, None]
    m = sink
    for s in scores:
        m = jnp.maximum(m, jnp.max(s, axis=-1, keepdims=True))
    denom = jnp.exp(sink - m)
    out = 0.0
    for s, v in zip(scores, values):
        p = jnp.exp(s - m)
        denom = denom + jnp.sum(p, axis=-1, keepdims=True)
        out = out + jnp.einsum('bkgts,bskd->bkgtd', p.astype(v.dtype), v).astype(jnp.float32)
    out = out / denom
    return out.transpose(0, 3, 1, 2, 4).astype(values[0].dtype)


def windowed_attention(q, k, v, kc, vc, sink):
    B, L, _, _ = q.shape
    nb = L // BLOCK
    qg = q.reshape(B, nb, BLOCK, N_KV_HEADS, GROUP, HEAD_DIM).transpose(1, 0, 2, 3, 4, 5)

    def band(t):
        tp = jnp.pad(t, ((0, 0), (BLOCK, BLOCK), (0, 0), (0, 0)))
        tp = tp.reshape(B, nb + 2, BLOCK, N_KV_HEADS, HEAD_DIM)
        tb = jnp.concatenate([tp[:, :nb], tp[:, 1:nb + 1], tp[:, 2:nb + 2]], axis=2)
        return tb.transpose(1, 0, 2, 3, 4)

    kb_all, vb_all = band(k), band(v)

    def block_fn(args):
        qb, kb, vb, b = args
        s_loc = jnp.einsum('btkgd,bskd->bkgts', qb, kb).astype(jnp.float32) * ATTN_SCALE
        qpos = b * BLOCK + jnp.arange(BLOCK)
        kpos = (b - 1) * BLOCK + jnp.arange(3 * BLOCK)
        valid = ((jnp.abs(qpos[:, None] - kpos[None, :]) <= WINDOW)
                 & (kpos[None, :] >= 0) & (kpos[None, :] < L))
        s_loc = jnp.where(valid, s_loc, NEG)
        s_ctx = jnp.einsum('btkgd,bckd->bkgtc', qb, kc).astype(jnp.float32) * ATTN_SCALE
        return sink_softmax([s_loc, s_ctx], [vb, vc], sink)

    o = lax.map(block_fn, (qg, kb_all, vb_all, jnp.arange(nb)))
    return o.transpose(1, 0, 2, 3, 4, 5).reshape(B, L, N_HEADS * HEAD_DIM)


def context_attention(qc, kc, vc, sink):
    B, Lc, _, _ = qc.shape
    qg = qc.reshape(B, Lc, N_KV_HEADS, GROUP, HEAD_DIM)
    s = jnp.einsum('bqkgd,bckd->bkgqc', qg, kc).astype(jnp.float32) * ATTN_SCALE
    return sink_softmax([s], [vc], sink).reshape(B, Lc, N_HEADS * HEAD_DIM)


def hyena_filter_fft(L, w1, b1, freq1, w2, b2, freq2, w_out, decay):
    f32 = jnp.float32
    t = jnp.linspace(0.0, 1.0, L, dtype=f32)[:, None]
    w = 2.0 * math.pi * jnp.arange(L, dtype=f32)[:, None] / L
    bands = jnp.linspace(1e-4, HY_BANDS - 1, HY_BANDS, dtype=f32)[None, :]
    z = jnp.concatenate([t, jnp.cos(bands * w), -jnp.sin(bands * w)], axis=-1)
    hdn = jnp.sin(freq1.astype(f32) * (z @ w1.astype(f32) + b1.astype(f32)))
    hdn = jnp.sin(freq2.astype(f32) * (hdn @ w2.astype(f32) + b2.astype(f32)))
    hh = hdn @ w_out.astype(f32)
    window = jnp.exp(-t * jnp.abs(decay.astype(f32))[None, :])
    h_fwd = hh[:, :D_MODEL] * window
    h_bwd = hh[:, D_MODEL:] * window
    k = jnp.concatenate([h_fwd, jnp.zeros((1, D_MODEL), f32), h_bwd[1:][::-1]], axis=0)
    k = k / jnp.sum(jnp.abs(k), axis=0, keepdims=True)
    return jnp.fft.rfft(k, n=2 * L, axis=0)


def hyena_mix(h, K, w_in, b_in, conv_w, conv_b, skip, w_out, b_out):
    B, L, _ = h.shape
    z = h @ w_in + b_in
    zp = jnp.pad(z, ((0, 0), (1, 1), (0, 0)))
    z = zp[:, :-2] * conv_w[0] + zp[:, 1:-1] * conv_w[1] + zp[:, 2:] * conv_w[2] + conv_b
    x0, x1, v = jnp.split(z, 3, axis=-1)
    u = (v * x1).astype(jnp.float32)
    U = jnp.fft.rfft(u, n=2 * L, axis=1)
    y = jnp.fft.irfft(U * K[None], n=2 * L, axis=1)[:, :L]
    y = (y + u * skip.astype(jnp.float32)).astype(h.dtype) * x0
    return y @ w_out + b_out


def swiglu(h, w1, w3, w2):
    return (jax.nn.silu(h @ w1) * (h @ w3)) @ w2


def setup_inputs(seed: int = 0) -> dict:
    key = jax.random.key(seed)
    ks = iter(jax.random.split(key, 48))
    f32 = jnp.float32

    def nrm(shape, scale):
        return scale * jax.random.normal(next(ks), shape, f32)

    D = D_MODEL
    NA = len(range(0, DEPTH, N_MIXERS))
    NB = len(range(1, DEPTH, N_MIXERS))
    qkv_w = (N_HEADS + 2 * N_KV_HEADS) * HEAD_DIM
    fast = -math.log(HY_DECAY_TARGET) / HY_FAST_PCT
    slow = -math.log(HY_DECAY_TARGET) / HY_SLOW_PCT
    decay_base = jnp.linspace(fast, slow, D, dtype=f32)[None, :]
    return {
        "x": nrm((BATCH, SEQ, D), 1.0),
        "c": nrm((BATCH, D), 1.0),
        "ctx": nrm((BATCH, CTX_LEN, D), 1.0),
        "c_ctx": nrm((D,), 1.0),
        "ada_w": nrm((DEPTH, D, 6 * D), 0.5 * D ** -0.5),
        "ada_b": nrm((DEPTH, 6 * D), 0.02),
        "norm1_g": 1.0 + nrm((DEPTH, D), 0.05),
        "norm2_g": 1.0 + nrm((DEPTH, D), 0.05),
        "attn_wqkv": nrm((NA, D, qkv_w), D ** -0.5),
        "attn_wo": nrm((NA, N_HEADS * HEAD_DIM, D), (N_HEADS * HEAD_DIM) ** -0.5),
        "attn_q_gain": 1.0 + nrm((NA, HEAD_DIM), 0.05),
        "attn_k_gain": 1.0 + nrm((NA, HEAD_DIM), 0.05),
        "attn_sink": nrm((NA, N_HEADS), 0.5),
        "hy_w_in": nrm((NB, D, 3 * D), D ** -0.5),
        "hy_b_in": nrm((NB, 3 * D), 0.02),
        "hy_conv_w": nrm((NB, HY_SHORT, 3 * D), HY_SHORT ** -0.5),
        "hy_conv_b": nrm((NB, 3 * D), 0.02),
        "hy_f_w1": nrm((NB, HY_EMB, HY_HIDDEN), HY_EMB ** -0.5),
        "hy_f_b1": nrm((NB, HY_HIDDEN), 0.5),
        "hy_f_freq1": 1.0 + nrm((NB, HY_HIDDEN), 0.1),
        "hy_f_w2": nrm((NB, HY_HIDDEN, HY_HIDDEN), HY_HIDDEN ** -0.5),
        "hy_f_b2": nrm((NB, HY_HIDDEN), 0.5),
        "hy_f_freq2": 1.0 + nrm((NB, HY_HIDDEN), 0.1),
        "hy_f_wout": nrm((NB, HY_HIDDEN, 2 * D), HY_HIDDEN ** -0.5),
        "hy_decay": decay_base * (1.0 + nrm((NB, D), 0.05)),
        "hy_skip": nrm((NB, D), 1.0),
        "hy_w_out": nrm((NB, D, D), D ** -0.5),
        "hy_b_out": nrm((NB, D), 0.02),
        "ffn_w1": nrm((DEPTH, D, D_FF), D ** -0.5),
        "ffn_w3": nrm((DEPTH, D, D_FF), D ** -0.5),
        "ffn_w2": nrm((DEPTH, D_FF, D), D_FF ** -0.5),
    }


def reference(x, c, ctx, c_ctx, ada_w, ada_b, norm1_g, norm2_g,
              attn_wqkv, attn_wo, attn_q_gain, attn_k_gain, attn_sink,
              hy_w_in, hy_b_in, hy_conv_w, hy_conv_b,
              hy_f_w1, hy_f_b1, hy_f_freq1, hy_f_w2, hy_f_b2, hy_f_freq2, hy_f_wout,
              hy_decay, hy_skip, hy_w_out, hy_b_out,
              ffn_w1, ffn_w3, ffn_w2):
    L = x.shape[1]
    Lc = ctx.shape[1]
    cos, sin = axial_rope_tables(L)
    last_ctx_reader = ((DEPTH - 1) // N_MIXERS) * N_MIXERS
    xc = ctx
    for i in range(DEPTH):
        upd = i < last_ctx_reader
        sh1, sc1, g1, sh2, sc2, g2 = ada_chunks(c, ada_w[i], ada_b[i])
        csh1, csc1, cg1, csh2, csc2, cg2 = ada_chunks(c_ctx[None, :], ada_w[i], ada_b[i])
        h = modulate(rms_norm(x, norm1_g[i]), sh1, sc1)
        if i % N_MIXERS == 0:
            a = i // N_MIXERS
            hc = modulate(rms_norm(xc, norm1_g[i]), csh1, csc1)
            kc, vc = kv_proj(hc, attn_wqkv[a], attn_k_gain[a])
            q = apply_rope(q_proj(h, attn_wqkv[a], attn_q_gain[a]), cos, sin)
            k, v = kv_proj(h, attn_wqkv[a], attn_k_gain[a])
            k = apply_rope(k, cos, sin)
            o = windowed_attention(q, k, v, kc, vc, attn_sink[a])
            x = x + g1 * (o @ attn_wo[a])
            if upd:
                qc = q_proj(hc, attn_wqkv[a], attn_q_gain[a])
                oc = context_attention(qc, kc, vc, attn_sink[a])
                xc = xc + cg1 * (oc @ attn_wo[a])
        else:
            j = i // N_MIXERS
            K = hyena_filter_fft(L, hy_f_w1[j], hy_f_b1[j], hy_f_freq1[j], hy_f_w2[j],
                                 hy_f_b2[j], hy_f_freq2[j], hy_f_wout[j], hy_decay[j])
            x = x + g1 * hyena_mix(h, K, hy_w_in[j], hy_b_in[j], hy_conv_w[j], hy_conv_b[j],
                                   hy_skip[j], hy_w_out[j], hy_b_out[j])
            if upd:
                hc = modulate(rms_norm(xc, norm1_g[i]), csh1, csc1)
                Kc = hyena_filter_fft(Lc, hy_f_w1[j], hy_f_b1[j], hy_f_freq1[j], hy_f_w2[j],
                                      hy_f_b2[j], hy_f_freq2[j], hy_f_wout[j], hy_decay[j])
                xc = xc + cg1 * hyena_mix(hc, Kc, hy_w_in[j], hy_b_in[j], hy_conv_w[j],
                                          hy_conv_b[j], hy_skip[j], hy_w_out[j], hy_b_out[j])
        x = x + g2 * swiglu(modulate(rms_norm(x, norm2_g[i]), sh2, sc2),
                            ffn_w1[i], ffn_w3[i], ffn_w2[i])
        if upd:
            xc = xc + cg2 * swiglu(modulate(rms_norm(xc, norm2_g[i]), csh2, csc2),
                                   ffn_w1[i], ffn_w3[i], ffn_w2[i])
    return x
```

```python
import contextlib
import math

import ml_dtypes
import numpy as np

import concourse.bass as bass
import concourse.mybir as mybir
from concourse.bass_utils import run_bass_kernel_spmd

F32 = mybir.dt.float32
BF16 = mybir.dt.bfloat16
AF = mybir.ActivationFunctionType
ALU = mybir.AluOpType
AX = mybir.AxisListType

D = 1024
L = 8192
LC = 256
DFF = 2816
NH = 16
NKV = 4
HD = 64
EPS = 1e-6
NCORES = 8
NT = L // 128
NJ = DFF // 128
NFFT = 2 * L


STRICT_SAME_ENGINE = True


class Key:
    def __init__(self, name=""):
        self.name = name
        self.lastw = None
        self.readers = {}
        self.dsem = None


class Buf(Key):
    def __init__(self, name, t):
        super().__init__(name)
        self.t = t

    def __getitem__(self, idx):
        return self.t[idx]


class Eng:
    def __init__(self, name, h, sem):
        self.name = name
        self.h = h
        self.sem = sem
        self.count = 0
        self.waited = {}


class DSem:
    def __init__(self, sem):
        self.sem = sem
        self.count = 0


class KB:
    def __init__(self, nc, n_dsem=84):
        self.nc = nc
        self.g = contextlib.ExitStack()
        self.engs = {}
        for name, h in (("pe", nc.tensor), ("act", nc.scalar), ("dve", nc.vector),
                        ("pool", nc.gpsimd), ("sp", nc.sync)):
            sem = self.g.enter_context(nc.semaphore("c_" + name))
            self.engs[name] = Eng(name, h, sem)
        self.bar_sem = self.g.enter_context(nc.semaphore("bar"))
        self.bar_count = 0
        self.dpool = [DSem(self.g.enter_context(nc.semaphore("d%d" % i))) for i in range(n_dsem)]
        self.dfree = list(self.dpool)
        self.keys = []
        self.uid = 0

    def key(self, name=""):
        k = Key(name)
        self.keys.append(k)
        return k

    def sbuf(self, stack, name, shape, dtype):
        self.uid += 1
        t = stack.enter_context(self.nc.sbuf_tensor("%s_%d" % (name, self.uid), list(shape), dtype))
        b = Buf(name, t)
        self.keys.append(b)
        return b

    def psum(self, stack, name, shape, dtype):
        self.uid += 1
        t = stack.enter_context(self.nc.psum_tensor("%s_%d" % (name, self.uid), list(shape), dtype))
        b = Buf(name, t)
        self.keys.append(b)
        return b

    def _wait(self, E, ev, same_ok):
        kind, src, val = ev
        if kind == "e":
            if src == E.name and (E.name == "pe" or (same_ok and not STRICT_SAME_ENGINE)):
                return
            if E.waited.get(src, 0) >= val:
                return
            E.waited[src] = val
            E.h.wait_ge(self.engs[src].sem, val)
        else:
            if E.waited.get(id(src), 0) >= val:
                return
            E.waited[id(src)] = val
            E.h.wait_ge(src.sem, val)

    def _deps(self, E, r, w, nowaw=False):
        for k in r:
            if k.lastw is not None:
                self._wait(E, k.lastw, False)
        for k in w:
            if k.lastw is not None and not nowaw:
                self._wait(E, k.lastw, True)
            for ev in k.readers.values():
                self._wait(E, ev, True)

    def _mark(self, ev, r, w):
        for k in w:
            k.lastw = ev
            k.readers = {}
        for k in r:
            if k not in w:
                k.readers[(ev[0], ev[1] if ev[0] == "e" else id(ev[1]))] = ev

    def op(self, en, fn, r=(), w=()):
        E = self.engs[en]
        self._deps(E, r, w)
        inst = fn(E.h)
        E.count += 1
        inst.then_inc(E.sem, 1)
        self._mark(("e", en, E.count), r, w)
        return inst

    def dma(self, qn, out, in_, r=(), w=(), nowaw=False, **kw):
        Q = self.engs[qn]
        self._deps(Q, r, w, nowaw=nowaw)
        k = (list(w) + list(r))[0]
        if k.dsem is None:
            k.dsem = self.dfree.pop()
        inst = Q.h.dma_start(out=out, in_=in_, **kw)
        k.dsem.count += 16
        inst.then_inc(k.dsem.sem, 16)
        self._mark(("d", k.dsem, k.dsem.count), r, w)
        return inst

    def barrier(self, final=False):
        sp = self.engs["sp"]
        for E in self.engs.values():
            if E.name != "sp" and E.count > 0:
                sp.h.wait_ge(E.sem, E.count)
        for ds in self.dpool:
            if ds.count > 0:
                sp.h.wait_ge(ds.sem, ds.count)
        if final:
            return
        for E in self.engs.values():
            if E.name != "sp" and E.count > 0:
                sp.h.sem_clear(E.sem)
        for ds in self.dpool:
            if ds.count > 0:
                sp.h.sem_clear(ds.sem)
                ds.count = 0
        if sp.count > 0:
            sp.h.sem_clear(sp.sem)
        self.bar_count += 1
        sp.h.nop().then_inc(self.bar_sem, 1)
        for E in self.engs.values():
            E.count = 0
            E.waited = {}
            if E.name != "sp":
                E.h.wait_ge(self.bar_sem, self.bar_count)
        for k in self.keys:
            k.lastw = None
            k.readers = {}
            if k.dsem is not None:
                k.dsem = None
        self.dfree = list(self.dpool)

    @contextlib.contextmanager
    def phase(self):
        st = contextlib.ExitStack()
        nkeys = len(self.keys)
        with st:
            yield st
            self.barrier()
        del self.keys[nkeys:]


def _bf(a):
    return np.ascontiguousarray(a.astype(np.float32)).astype(ml_dtypes.bfloat16)


_CONST_CACHE = {}


def host_consts():
    if _CONST_CACHE:
        return _CONST_CACHE
    c = {}
    c["ident_bf"] = _bf(np.eye(128))
    c["ident_f"] = np.eye(128, dtype=np.float32)
    c["ones_bf"] = _bf(np.ones((128, 128)))
    rows = L // 64
    row = np.repeat(np.arange(rows, dtype=np.float32), 64)
    col = np.tile(np.arange(64, dtype=np.float32), rows)
    inv = (np.float32(10000.0) ** (-np.arange(16, dtype=np.float32) / np.float32(16))).astype(np.float32)
    ang = np.concatenate([row[:, None] * inv, col[:, None] * inv], axis=-1).astype(np.float32)
    c["rope_cos"] = np.ascontiguousarray(np.cos(ang).astype(np.float32).reshape(NT, 128, 32).transpose(1, 0, 2))
    c["rope_sin"] = np.ascontiguousarray(np.sin(ang).astype(np.float32).reshape(NT, 128, 32).transpose(1, 0, 2))
    jj = np.arange(128)[:, None]
    ii = np.arange(128)[None, :]
    mprev = np.where(jj >= ii, 0.0, -30000.0)
    mnext = np.where(jj <= ii, 0.0, -30000.0)
    mb = np.stack([np.tile(mprev, (1, 4)), np.tile(mnext, (1, 4))], axis=1)
    c["mask_bias"] = _bf(mb)
    n = np.arange(NFFT)
    m = np.where(n <= L, n, NFFT - n).astype(np.float64)
    t = (m / (L - 1)).astype(np.float32)
    w = (2.0 * np.pi * m / L)
    bands = np.linspace(1e-4, 15.0, 16, dtype=np.float32).astype(np.float64)
    zt = np.concatenate([t[None, :].astype(np.float64), np.cos(bands[:, None] * w[None, :]),
                         -np.sin(bands[:, None] * w[None, :])], axis=0)
    c["hy_zt"] = np.ascontiguousarray(zt.astype(np.float32))
    c["hy_tpos"] = np.ascontiguousarray(t.reshape(128, 128))
    a = np.arange(128)[:, None]
    k1 = np.arange(64)[None, :]
    ph = 2 * np.pi * a * (2 * k1 + 1) / 256.0
    c["fft_f1"] = _bf(np.concatenate([np.cos(ph), -np.sin(ph)], axis=1))
    b_ = np.arange(128)[None, :, None]
    kk = (np.arange(64)[:, None, None] + 128 * np.arange(128)[None, None, :])
    ph = 2 * np.pi * b_ * (2 * kk + 1) / (2.0 * NFFT)
    c["fft_h2"] = _bf(np.stack([np.cos(ph), -np.sin(ph), np.sin(ph)], axis=2))
    k2 = np.arange(128)[:, None]
    bb = np.arange(128)[None, :]
    ps_ = 2 * np.pi * bb * k2 / 128.0
    c["fft_r1"] = _bf(np.concatenate([np.cos(ps_), np.sin(ps_)], axis=1))
    c["fft_r2"] = _bf(np.concatenate([-np.sin(ps_), np.cos(ps_)], axis=1))
    nn = (128 * np.arange(64)[None, None, :] + np.arange(128)[None, :, None])
    ch = 2 * np.pi * nn * (2 * np.arange(64)[:, None, None] + 1) / (2.0 * NFFT)
    c["fft_h4"] = _bf(np.stack([np.cos(ch), -np.sin(ch)], axis=2) * (2.0 / NFFT))
    _CONST_CACHE.update(c)
    return c


class Prog:
    def __init__(self, stop_after=None, taps=(), start_at=None, mode="full"):
        self.mode = mode
        self.layers = {"full": [0, 1], "l0": [0], "l1": [1]}[mode]
        if mode == "l1":
            start_at = "hy"
        self.start_at = start_at
        self.stop_after = stop_after
        self.taps = set(taps)
        nc = bass.Bass("TRN2", target_bir_lowering=False)
        self.nc = nc
        self.kb = KB(nc)
        self.din = {}
        self.outs = []

    def ext_in(self, name, shape, dtype=F32):
        t = self.nc.dram_tensor(name, list(shape), dtype, kind="ExternalInput")
        self.din[name] = t
        return t

    def scratch(self, name, shape, dtype=F32):
        kind = "ExternalOutput" if name in self.taps else "Internal"
        t = self.nc.dram_tensor(name, list(shape), dtype, kind=kind)
        if name in self.taps:
            self.outs.append(name)
        return t

    def declare_inputs(self):
        e = self.ext_in
        ly = self.layers
        self.x = e("x", [L, D])
        self.c = e("c", [D])
        self.ada_w = {i: e("ada_w%d" % i, [D, 6 * D]) for i in ly}
        self.ada_b = {i: e("ada_b%d" % i, [6 * D]) for i in ly}
        self.norm1_g = {i: e("norm1_g%d" % i, [D]) for i in ly}
        self.norm2_g = {i: e("norm2_g%d" % i, [D]) for i in ly}
        self.ffn_w1 = {i: e("ffn_w1_%d" % i, [D, DFF]) for i in ly}
        self.ffn_w3 = {i: e("ffn_w3_%d" % i, [D, DFF]) for i in ly}
        self.ffn_w2 = {i: e("ffn_w2_%d" % i, [DFF, D]) for i in ly}
        self.ident_bf = e("ident_bf", [128, 128], BF16)
        self.ident_f = e("ident_f", [128, 128], F32)
        if 0 in ly:
            self.ctx = e("ctx", [LC, D])
            self.c_ctx = e("c_ctx", [D])
            self.ones_bf = e("ones_bf", [128, 128], BF16)
            self.rope_cos = e("rope_cos", [128, NT, 32])
            self.rope_sin = e("rope_sin", [128, NT, 32])
            self.mask_bias = e("mask_bias", [128, 2, 512], BF16)
            self.attn_wqkv = e("attn_wqkv", [1, D, 1536])
            self.attn_wo = e("attn_wo", [1, D, D])
            self.attn_q_gain = e("attn_q_gain", [1, HD])
            self.attn_k_gain = e("attn_k_gain", [1, HD])
            self.attn_sink = e("attn_sink", [1, NH])
        if 1 in ly:
            self.hy_w_in = e("hy_w_in", [1, D, 3 * D])
            self.hy_b_in = e("hy_b_in", [1, 3 * D])
            self.hy_conv_w = e("hy_conv_w", [1, 3, 3 * D])
            self.hy_conv_b = e("hy_conv_b", [1, 3 * D])
            self.hy_f_w1 = e("hy_f_w1", [1, 33, 64])
            self.hy_f_b1 = e("hy_f_b1", [1, 64])
            self.hy_f_freq1 = e("hy_f_freq1", [1, 64])
            self.hy_f_w2 = e("hy_f_w2", [1, 64, 64])
            self.hy_f_b2 = e("hy_f_b2", [1, 64])
            self.hy_f_freq2 = e("hy_f_freq2", [1, 64])
            self.hy_f_wout = e("hy_f_wout", [1, 64, 2 * D])
            self.hy_decay = e("hy_decay", [1, D])
            self.hy_skip = e("hy_skip", [1, D])
            self.hy_w_out = e("hy_w_out", [1, D, D])
            self.hy_b_out = e("hy_b_out", [1, D])
            self.hy_zt = e("hy_zt", [33, NFFT])
            self.hy_tpos = e("hy_tpos", [128, 128])
            self.fft_f1 = e("fft_f1", [128, 128], BF16)
            self.fft_h2 = e("fft_h2", [64, 128, 3, 128], BF16)
            self.fft_r1 = e("fft_r1", [128, 256], BF16)
            self.fft_r2 = e("fft_r2", [128, 256], BF16)
            self.fft_h4 = e("fft_h4", [64, 128, 2, 64], BF16)

    def build(self):
        kb = self.kb
        nc = self.nc
        self.declare_inputs()
        g = kb.g
        self.idb = kb.sbuf(g, "idb", [128, 128], BF16)
        self.idf = kb.sbuf(g, "idf", [128, 128], F32)
        self.mT = [kb.sbuf(g, "mT%d" % i, [128, 48], F32) for i in range(2)]
        self.cmT = kb.sbuf(g, "cmT", [128, 16], F32)
        self.A = [[kb.sbuf(g, "A%d_%d" % (i, j), [128, 8], F32) for j in range(2)] for i in range(2)]
        self.cA = kb.sbuf(g, "cA", [128, 8], F32)
        self.gb_d = [[self.nc.dram_tensor("gb_d%d_%d" % (i, j), [128, D], F32, kind="Internal") for j in range(2)] for i in range(2)]
        kb.dma("sp", self.idb[:], self.ident_bf.ap()[:, :], w=[self.idb])
        kb.dma("sp", self.idf[:], self.ident_f.ap()[:, :], w=[self.idf])

        self.ph_ada()
        if self.stop_after == "ada":
            return self.finish()
        nc_ = self.nc
        SA = nc_.dram_tensor("SA", [L * D], F32, kind="Internal")
        SC = nc_.dram_tensor("SC", [L * D], F32, kind="Internal")
        tok = lambda t: t.ap().rearrange("(l d) -> l d", d=D)
        chn = lambda t: t.ap().rearrange("(d l) -> d l", l=L)
        if self.start_at == "hy":
            x1v = self.x.ap()
        else:
            self.ph_attn(self.x.ap(), tok(SA))
            if self.stop_after == "attn":
                return self.finish()
            if self.mode == "l0":
                x1t = nc_.dram_tensor("out", [L, D], F32, kind="ExternalOutput")
                x1v = x1t.ap()
            else:
                SB = nc_.dram_tensor("SB", [L * D], F32, kind="Internal")
                x1v = tok(SB)
            self.ph_ffn(0, tok(SA), x1v)
            if self.stop_after == "ffn0" or self.mode == "l0":
                return self.finish()
        self.u32_v = chn(SA)
        self.x0_v = chn(SC)
        self.ubf_v = self.scratch("ubf_d", [D, L], BF16).ap()
        self.ph_hy_in(x1v)
        if self.stop_after == "hyin":
            return self.finish()
        self.vT_v = self.scratch("vT_d", [D, L], BF16).ap()
        self.ph_hy_fft()
        if self.stop_after == "hyfft":
            return self.finish()
        self.ph_hy_out(x1v, tok(SA))
        if self.stop_after == "hyout":
            return self.finish()
        self.out = self.nc.dram_tensor("out", [L, D], F32, kind="ExternalOutput")
        self.ph_ffn(getattr(self, "last_li", 1), tok(SA), self.out.ap())
        return self.finish()

    def finish(self):
        self.kb.barrier(final=True)
        self.kb.g.close()
        return self.nc

    def ph_ada(self):
        kb = self.kb
        with kb.phase() as st:
            cT = kb.sbuf(st, "cT", [128, 16], F32)
            sT = kb.sbuf(st, "sT", [128, 16], F32)
            scb = kb.sbuf(st, "scb", [128, 16, 128], F32)
            wbuf = [kb.sbuf(st, "adaw%d" % i, [128, 8, 512], F32) for i in range(2)]
            bb = [kb.sbuf(st, "adab%d" % i, [128, 512], F32) for i in range(2)]
            B6 = kb.sbuf(st, "B6", [128, 6 * D], F32)
            CB = kb.sbuf(st, "CB", [128, 2 * D], F32)
            tmp = kb.sbuf(st, "adatmp", [128, 48, 128], F32)
            ngT = kb.sbuf(st, "ngT", [128, 4, 8], F32)
            one = kb.sbuf(st, "one", [128, 8], F32)
            ps = [kb.psum(st, "adaps%d" % i, [128, 512], F32) for i in range(4)]
            nc = self.nc
            with nc.allow_non_contiguous_dma(reason="tiny per-partition vector loads"):
                kb.dma("sp", cT[:, 0:8], self.c.ap().rearrange("(c p) -> p c", p=128), w=[cT])
                if 0 in self.layers:
                    kb.dma("sp", cT[:, 8:16], self.c_ctx.ap().rearrange("(c p) -> p c", p=128), w=[cT], nowaw=True)
                else:
                    kb.dma("sp", cT[:, 8:16], self.c.ap().rearrange("(c p) -> p c", p=128), w=[cT], nowaw=True)
                for i in self.layers:
                    kb.dma("sp", ngT[:, i, :], self.norm1_g[i].ap().rearrange("(c p) -> p c", p=128), w=[ngT], nowaw=True)
                    kb.dma("sp", ngT[:, 2 + i, :], self.norm2_g[i].ap().rearrange("(c p) -> p c", p=128), w=[ngT], nowaw=True)
            kb.op("act", lambda h: h.activation(out=sT[:], in_=cT[:], func=AF.Silu), r=[cT], w=[sT])
            kb.op("dve", lambda h: h.tensor_copy(out=scb[:], in_=sT[:].unsqueeze(2).to_broadcast([128, 16, 128])),
                  r=[sT], w=[scb])
            cnt = 0
            for i in self.layers:
                for gidx in range(12):
                    wb = wbuf[cnt % 2]
                    b_ = bb[cnt % 2]
                    p = ps[cnt % 2]
                    cols = slice(gidx * 512, (gidx + 1) * 512)
                    kb.dma("sp", wb[:], self.ada_w[i].ap()[:, cols].rearrange("(kc p) n -> p kc n", p=128), w=[wb])
                    kb.dma("sp", b_[:], self.ada_b[i].ap()[cols].partition_broadcast(128), w=[b_])
                    for kc in range(8):
                        kb.op("pe", lambda h, kc=kc: h.matmul(p[:], lhsT=scb[:, kc, :], rhs=wb[:, kc, :],
                                                              start=(kc == 0), stop=(kc == 7)),
                              r=[scb, wb], w=[p])
                    kb.op("dve", lambda h: h.tensor_tensor(out=B6[:, cols], in0=p[:], in1=b_[:], op=ALU.add),
                          r=[p, b_], w=[B6])
                    if i == 0 and gidx < 4:
                        p2 = ps[2 + cnt % 2]
                        for kc in range(8):
                            kb.op("pe", lambda h, kc=kc: h.matmul(p2[:], lhsT=scb[:, 8 + kc, :], rhs=wb[:, kc, :],
                                                                  start=(kc == 0), stop=(kc == 7)),
                                  r=[scb, wb], w=[p2])
                        kb.op("dve", lambda h: h.tensor_tensor(out=CB[:, cols], in0=p2[:], in1=b_[:], op=ALU.add),
                              r=[p2, b_], w=[CB])
                    cnt += 1
                kb.op("dve", lambda h: h.tensor_tensor(out=tmp[:], in0=B6[:].rearrange("p (j n) -> p j n", n=128),
                                                       in1=self.idf[:].unsqueeze(1).to_broadcast([128, 48, 128]),
                                                       op=ALU.mult), r=[B6, self.idf], w=[tmp])
                kb.op("dve", lambda h: h.tensor_reduce(out=self.mT[i][:], in_=tmp[:], axis=AX.X, op=ALU.add),
                      r=[tmp], w=[self.mT[i]])
                kb.dma("sp", self.gb_d[i][0].ap()[:, :], B6[:, 2 * D:3 * D], r=[B6])
                kb.dma("sp", self.gb_d[i][1].ap()[:, :], B6[:, 5 * D:6 * D], r=[B6])
                if i == 0:
                    kb.op("dve", lambda h: h.tensor_tensor(out=tmp[:, 0:16, :], in0=CB[:].rearrange("p (j n) -> p j n", n=128),
                                                           in1=self.idf[:].unsqueeze(1).to_broadcast([128, 16, 128]),
                                                           op=ALU.mult), r=[CB, self.idf], w=[tmp])
                    kb.op("dve", lambda h: h.tensor_reduce(out=self.cmT[:], in_=tmp[:, 0:16, :], axis=AX.X, op=ALU.add),
                          r=[tmp], w=[self.cmT])
                for j in range(2):
                    sc = self.mT[i][:, (8 + 24 * j):(16 + 24 * j)]
                    kb.op("dve", lambda h: h.tensor_scalar(out=one[:], in0=sc, scalar1=1.0, scalar2=None, op0=ALU.add),
                          r=[self.mT[i]], w=[one])
                    kb.op("dve", lambda h: h.tensor_tensor(out=self.A[i][j][:], in0=one[:], in1=ngT[:, 2 * j + i, :], op=ALU.mult),
                          r=[one, ngT], w=[self.A[i][j]])
                if i == 0:
                    kb.op("dve", lambda h: h.tensor_scalar(out=one[:], in0=self.cmT[:, 8:16], scalar1=1.0, scalar2=None, op0=ALU.add),
                          r=[self.cmT], w=[one])
                    kb.op("dve", lambda h: h.tensor_tensor(out=self.cA[:], in0=one[:], in1=ngT[:, 0, :], op=ALU.mult),
                          r=[one, ngT], w=[self.cA])
            if "dbg_ada" in self.taps:
                dbg = self.scratch("dbg_ada", [128, 48 * 2 + 16 + 8 * 5])
                o = 0
                for t_, n in ((self.mT[0], 48), (self.mT[1], 48), (self.cmT, 16), (self.A[0][0], 8), (self.A[0][1], 8),
                              (self.A[1][0], 8), (self.A[1][1], 8), (self.cA, 8)):
                    kb.dma("sp", dbg.ap()[:, o:o + n], t_[:], r=[t_])
                    o += n


    def ph_attn(self, xin, xout):
        kb = self.kb
        nc = self.nc
        with kb.phase() as st:
            wqkv = kb.sbuf(st, "wqkv", [128, 8, 1536], BF16)
            wo = kb.sbuf(st, "wo", [64, NH, D], BF16)
            kb.dma("pool", wqkv[:], self.attn_wqkv.ap()[0].rearrange("(kc p) n -> p kc n", p=128), w=[wqkv])
            kb.dma("pool", wo[:], self.attn_wo.ap()[0].rearrange("(h d) n -> d h n", d=64), w=[wo])
            gq = kb.sbuf(st, "gq", [128, 20, 64], F32)
            for h_ in range(20):
                src = self.attn_q_gain if h_ < 16 else self.attn_k_gain
                kb.dma("sp", gq[:, h_, :], src.ap()[0].partition_broadcast(128), w=[gq], nowaw=True)
            esink = kb.sbuf(st, "esink", [128, NH], F32)
            kb.dma("sp", esink[:], self.attn_sink.ap()[0].partition_broadcast(128), w=[esink])
            kb.op("act", lambda h: h.activation(out=esink[:], in_=esink[:], func=AF.Exp), r=[esink], w=[esink])
            cosT = kb.sbuf(st, "cosT", [128, NT, 32], F32)
            sinT = kb.sbuf(st, "sinT", [128, NT, 32], F32)
            kb.dma("sp", cosT[:], self.rope_cos.ap()[:, :, :], w=[cosT])
            kb.dma("sp", sinT[:], self.rope_sin.ap()[:, :, :], w=[sinT])
            MB = kb.sbuf(st, "MB", [128, 2, 512], BF16)
            kb.dma("sp", MB[:], self.mask_bias.ap()[:, :, :], w=[MB])
            onesb = kb.sbuf(st, "onesb", [128, 128], BF16)
            kb.dma("sp", onesb[:], self.ones_bf.ap()[:, :], w=[onesb])

            NXB = 4
            xt = [kb.sbuf(st, "xt%d" % i, [128, D], F32) for i in range(NXB)]
            hT = [kb.sbuf(st, "hT%d" % i, [128, 8, 128], BF16) for i in range(2)]
            junk1 = kb.sbuf(st, "junk", [128, D], BF16)
            PB = [kb.psum(st, "pb%d" % i, [128, 512], F32) for i in range(8)]
            pT0 = PB[0]
            ws = []
            for i in range(2):
                ws.append(dict(ss=kb.sbuf(st, "ss%d" % i, [128, 1], F32), rs=kb.sbuf(st, "rs%d" % i, [128, 1], F32),
                               rstd=kb.sbuf(st, "rstd%d" % i, [128, 1], F32), junk=junk1,
                               xs=kb.sbuf(st, "xs%d" % i, [128, D], BF16)))
            sq = kb.sbuf(st, "sq", [128, 1280], F32)
            ssq = kb.sbuf(st, "ssq", [128, 20], F32)
            rsq = kb.sbuf(st, "rsq", [128, 20], F32)
            rstq = kb.sbuf(st, "rstq", [128, 20], F32)
            qn = kb.sbuf(st, "qn", [128, 1280], F32)
            t1 = kb.sbuf(st, "t1", [128, 640], F32)
            t2 = kb.sbuf(st, "t2", [128, 640], F32)
            t3 = kb.sbuf(st, "t3", [128, 640], F32)
            t4 = kb.sbuf(st, "t4", [128, 640], F32)
            qr = [kb.sbuf(st, "qr%d" % i, [128, 1280], BF16) for i in range(2)]
            QT = [kb.sbuf(st, "QT%d" % i, [64, NH * 128], BF16) for i in range(3)]
            NKR = 4
            KT = [kb.sbuf(st, "KT%d" % i, [64, NKV * 128], BF16) for i in range(NKR)]
            VV = [kb.sbuf(st, "VV%d" % i, [128, NKV * 64], BF16) for i in range(NKR)]
            KTc = [kb.sbuf(st, "KTc%d" % i, [64, NKV * 128], BF16) for i in range(2)]
            VVc = [kb.sbuf(st, "VVc%d" % i, [128, NKV * 64], BF16) for i in range(2)]
            PT = [kb.sbuf(st, "PT%d" % i, [128, 5 * 512], BF16) for i in range(2)]
            rr = kb.sbuf(st, "rr", [64, 512], F32)
            OT = [kb.sbuf(st, "OT%d" % i, [64, NH * 128], BF16) for i in range(2)]
            ot = [kb.sbuf(st, "ot%d" % i, [128, 512], F32) for i in range(3)]
            xin_t = xin.rearrange("(n p) d -> n p d", p=128)
            xout_t = xout.rearrange("(n p) d -> n p d", p=128)
            ctx_t = self.ctx.ap().rearrange("(n p) d -> n p d", p=128)
            g1b = kb.sbuf(st, "g1b", [128, D], F32)
            kb.dma("sp", g1b[:], self.gb_d[0][0].ap()[:, :], w=[g1b])
            xslot = lambda n: xt[(n + 2) % NXB]

            def bf(pb, shape_str=None, **kw):
                a = pb[:].bitcast(BF16)
                return a

            def stage_a(xtile, xkey, A, S, w_, hbuf, cs_idx, kdst, vdst, qdst, qrb):
                w_["pT_ap"] = bf(pT0).rearrange("p (c t) -> p c t", t=128)
                w_["pT_key"] = pT0
                only_kv = qdst is None
                cgs = (2,) if only_kv else (0, 1, 2)
                h0 = 16 if only_kv else 0
                QB = lambda cg: PB[1 + cg]

                def p1():
                    self.norm_T_a1(xtile, [xkey], w_)

                def p2():
                    self.norm_T_a2(w_)

                def p3():
                    self.norm_T_b(lambda c: hbuf[:, c, :], hbuf, A, S[0], [A, S[1]], w_)

                def p4():
                    for cg in cgs:
                        pq = QB(cg)
                        for kc in range(8):
                            kb.op("pe", lambda h, kc=kc: h.matmul(pq[:], lhsT=hbuf[:, kc, :], rhs=wqkv[:, kc, cg * 512:(cg + 1) * 512],
                                                                  start=(kc == 0), stop=(kc == 7)), r=[hbuf, wqkv], w=[pq])

                def p5():
                    for cg in cgs:
                        pq = QB(cg)
                        n = 512 if cg < 2 else 256
                        kb.op("act", lambda h: h.activation(out=sq[:, cg * 512:cg * 512 + n], in_=pq[:, 0:n], func=AF.Square),
                              r=[pq], w=[sq])
                    kb.op("dve", lambda h: h.tensor_reduce(out=ssq[:, h0:20], in_=sq[:, h0 * 64:1280].rearrange("p (h d) -> p h d", d=64),
                                                           axis=AX.X, op=ALU.add), r=[sq], w=[ssq])
                    kb.op("act", lambda h: h.activation(out=rsq[:, h0:20], in_=ssq[:, h0:20], func=AF.Sqrt, bias=EPS, scale=1.0 / HD),
                          r=[ssq], w=[rsq])
                    kb.op("dve", lambda h: h.reciprocal(out=rstq[:, h0:20], in_=rsq[:, h0:20]), r=[rsq], w=[rstq])
                    for cg in cgs:
                        pq = QB(cg)
                        nh = 8 if cg < 2 else 4
                        hs = cg * 8
                        kb.op("dve", lambda h: h.tensor_tensor(out=qn[:, hs * 64:(hs + nh) * 64].rearrange("p (h d) -> p h d", d=64),
                                                               in0=pq[:, 0:nh * 64].rearrange("p (h d) -> p h d", d=64),
                                                               in1=rstq[:, hs:hs + nh].unsqueeze(2).to_broadcast([128, nh, 64]),
                                                               op=ALU.mult), r=[pq, rstq], w=[qn])
                    kb.op("act", lambda h: h.copy(out=vdst[:], in_=QB(2)[:, 256:512]), r=[QB(2)], w=[vdst])
                    qn3 = qn[:, h0 * 64:1280].rearrange("p (h d) -> p h d", d=64)
                    kb.op("pool", lambda h: h.tensor_tensor(out=qn3, in0=qn3, in1=gq[:, h0:20, :], op=ALU.mult), r=[qn, gq], w=[qn])
                    qrv = qrb[:, h0 * 64:1280]
                    if cs_idx is None:
                        kb.op("dve", lambda h: h.tensor_copy(out=qrv, in_=qn[:, h0 * 64:1280]), r=[qn], w=[qrb])
                    else:
                        q4 = qn[:, :].rearrange("p (h two j) -> p h two j", two=2, j=32)
                        o4 = qrb[:, :].rearrange("p (h two j) -> p h two j", two=2, j=32)
                        x1 = q4[:, :, 0, :]
                        x2 = q4[:, :, 1, :]
                        cb = cosT[:, cs_idx, :].unsqueeze(1).to_broadcast([128, 20, 32])
                        sb_ = sinT[:, cs_idx, :].unsqueeze(1).to_broadcast([128, 20, 32])
                        v3 = lambda t_: t_[:, :].rearrange("p (h j) -> p h j", j=32)
                        kb.op("dve", lambda h: h.tensor_tensor(out=v3(t1), in0=x1, in1=cb, op=ALU.mult), r=[qn, cosT], w=[t1])
                        kb.op("pool", lambda h: h.tensor_tensor(out=v3(t2), in0=x2, in1=sb_, op=ALU.mult), r=[qn, sinT], w=[t2])
                        kb.op("dve", lambda h: h.tensor_tensor(out=v3(t3), in0=x1, in1=sb_, op=ALU.mult), r=[qn, sinT], w=[t3])
                        kb.op("pool", lambda h: h.tensor_tensor(out=v3(t4), in0=x2, in1=cb, op=ALU.mult), r=[qn, cosT], w=[t4])
                        kb.op("dve", lambda h: h.tensor_tensor(out=o4[:, :, 0, :], in0=v3(t1), in1=v3(t2), op=ALU.subtract),
                              r=[t1, t2], w=[qrb])
                        kb.op("pool", lambda h: h.tensor_tensor(out=o4[:, :, 1, :], in0=v3(t3), in1=v3(t4), op=ALU.add),
                              r=[t3, t4], w=[qrb])

                def p6():
                    pb = PB[0]
                    pv = bf(pb)
                    if not only_kv:
                        for half in range(2):
                            for hh in range(8):
                                hd = half * 8 + hh
                                kb.op("pe", lambda h, hd=hd, hh=hh: h.transpose(pv[0:64, hh * 128:(hh + 1) * 128],
                                                                              qrb[:, hd * 64:(hd + 1) * 64], self.idb[:]),
                                      r=[qrb, self.idb], w=[pb])
                            kb.op("act" if half == 0 else "dve",
                                  (lambda h: h.copy(out=qdst[:, 0:1024], in_=pv[0:64, 0:1024])) if half == 0 else
                                  (lambda h: h.tensor_copy(out=qdst[:, 1024:2048], in_=pv[0:64, 0:1024])),
                                  r=[pb], w=[qdst])
                    for hh in range(4):
                        kb.op("pe", lambda h, hh=hh: h.transpose(pv[0:64, hh * 128:(hh + 1) * 128],
                                                                qrb[:, (16 + hh) * 64:(17 + hh) * 64], self.idb[:]),
                              r=[qrb, self.idb], w=[pb])
                    kb.op("act", lambda h: h.copy(out=kdst[:], in_=pv[0:64, 0:512]), r=[pb], w=[kdst])

                return [p1, p2, p3, p4, p5, p6]

            sbanks = [PB[4], PB[5]]
            sb_cnt = [0]

            def chunks_of(t):
                ch = [(KT[t % NKR], VV[t % NKR], None)]
                if t > 0:
                    ch.append((KT[(t - 1) % NKR], VV[(t - 1) % NKR], 0))
                if t < NT - 1:
                    ch.append((KT[(t + 1) % NKR], VV[(t + 1) % NKR], 1))
                ch.append((KTc[0], VVc[0], None))
                ch.append((KTc[1], VVc[1], None))
                return ch

            def scores(t, g):
                qt = QT[t % 3]
                pt = PT[g % 2]
                qcols = qt[:, g * 512:(g + 1) * 512]
                for i, (kt_, v_, m) in enumerate(chunks_of(t)):
                    ps_ = sbanks[sb_cnt[0] % 2]
                    sb_cnt[0] += 1
                    kb.op("pe", lambda h: h.matmul(ps_[:], lhsT=kt_[:, g * 128:(g + 1) * 128], rhs=qcols,
                                                   start=True, stop=(m is None)), r=[kt_, qt], w=[ps_])
                    if m is not None:
                        kb.op("pe", lambda h: h.matmul(ps_[:], lhsT=self.idb[:], rhs=MB[:, m, :], start=False, stop=True),
                              r=[self.idb, MB], w=[ps_])
                    kb.op("act", lambda h: h.activation(out=pt[:, i * 512:(i + 1) * 512], in_=ps_[:], func=AF.Exp,
                                                        scale=float(HD) ** -0.5), r=[ps_], w=[pt])

            def rest(t, g):
                o_t = OT[t % 2]
                pt = PT[g % 2]
                chunks = chunks_of(t)
                nchk = len(chunks)
                pden = PB[6]
                pov = PB[7]
                for i in range(nchk):
                    kb.op("pe", lambda h: h.matmul(pden[:], lhsT=onesb[:], rhs=pt[:, i * 512:(i + 1) * 512],
                                                   start=(i == 0), stop=(i == nchk - 1)), r=[onesb, pt], w=[pden])
                for i, (kt_, v_, m) in enumerate(chunks):
                    kb.op("pe", lambda h: h.matmul(pov[0:64, :], lhsT=v_[:, g * 64:(g + 1) * 64], rhs=pt[:, i * 512:(i + 1) * 512],
                                                   start=(i == 0), stop=(i == nchk - 1)), r=[v_, pt], w=[pov])
                kb.op("dve", lambda h: h.tensor_tensor(out=rr[:, :].rearrange("p (h q) -> p h q", q=128),
                                                       in0=pden[0:64, :].rearrange("p (h q) -> p h q", q=128),
                                                       in1=esink[0:64, 4 * g:4 * g + 4].unsqueeze(2).to_broadcast([64, 4, 128]),
                                                       op=ALU.add), r=[pden, esink], w=[rr])
                kb.op("dve", lambda h: h.reciprocal(out=rr[:], in_=rr[:]), r=[rr], w=[rr])
                kb.op("dve", lambda h: h.tensor_tensor(out=o_t[:, g * 512:(g + 1) * 512], in0=pov[0:64, :], in1=rr[:], op=ALU.mult),
                      r=[pov, rr], w=[o_t])

            def wo_fn(t):
                o_t = OT[t % 2]
                b = xslot(t)
                for hf in range(2):
                    pw = PB[4] if hf == 0 else PB[5]
                    cs = slice(hf * 512, (hf + 1) * 512)
                    for hd in range(NH):
                        kb.op("pe", lambda h, hd=hd: h.matmul(pw[:], lhsT=o_t[:, hd * 128:(hd + 1) * 128], rhs=wo[:, hd, cs],
                                                              start=(hd == 0), stop=(hd == NH - 1)), r=[o_t, wo], w=[pw])
                    o_ = ot[(t * 2 + hf) % 3]
                    kb.op("dve", lambda h: h.tensor_tensor(out=o_[:], in0=pw[:], in1=g1b[:, cs], op=ALU.mult), r=[pw, g1b], w=[o_])
                    kb.op("pool", lambda h: h.tensor_tensor(out=o_[:], in0=o_[:], in1=b[:, cs], op=ALU.add), r=[o_, b], w=[o_])
                    kb.dma("sp", xout_t[t][:, cs], o_[:], r=[o_])

            for ci in range(2):
                b = xt[ci]
                kb.dma("sp", b[:], ctx_t[ci], w=[b])
                for f in stage_a(b[:], b, self.cA, (self.cmT[:, 0:8], self.cmT), ws[ci % 2], hT[ci % 2], None,
                                 KTc[ci], VVc[ci], None, qr[ci % 2]):
                    f()
            kb.dma("sp", xt[2][:], xin_t[0], w=[xt[2]])
            noop = [lambda: None] * 6
            for n in range(NT + 2):
                if n + 1 < NT:
                    kb.dma("sp", xslot(n + 1)[:], xin_t[n + 1], w=[xslot(n + 1)])
                if n < NT:
                    b = xslot(n)
                    A_ = stage_a(b[:], b, self.A[0][0], (self.mT[0][:, 0:8], self.mT[0]), ws[n % 2], hT[n % 2], n,
                                 KT[n % NKR], VV[n % NKR], QT[n % 3], qr[n % 2])
                else:
                    A_ = noop
                tb = n - 2
                has_b = 0 <= tb < NT
                A_[0]()
                if has_b:
                    scores(tb, 0)
                A_[1]()
                A_[2]()
                if has_b:
                    scores(tb, 1)
                    rest(tb, 0)
                A_[3]()
                A_[4]()
                if has_b:
                    scores(tb, 2)
                    rest(tb, 1)
                    scores(tb, 3)
                    rest(tb, 2)
                    rest(tb, 3)
                A_[5]()
                if has_b:
                    wo_fn(tb)

    def ph_hy_in(self, xin):
        kb = self.kb
        with kb.phase() as st:
            hT = kb.sbuf(st, "hTall", [128, 8, L], BF16)
            hk = [kb.key("hk%d" % i) for i in range(16)]
            xt = [kb.sbuf(st, "xt%d" % i, [128, D], F32) for i in range(2)]
            junk1 = kb.sbuf(st, "junk", [128, D], BF16)
            ws = []
            for i in range(2):
                pk = kb.psum(st, "pT%d" % i, [128, 8, 128], BF16)
                ws.append(dict(ss=kb.sbuf(st, "ss%d" % i, [128, 1], F32), rs=kb.sbuf(st, "rs%d" % i, [128, 1], F32),
                               rstd=kb.sbuf(st, "rstd%d" % i, [128, 1], F32), junk=junk1,
                               xs=kb.sbuf(st, "xs%d" % i, [128, D], BF16), pT_key=pk, pT_ap=pk[:]))
            vb = kb.sbuf(st, "hyvec", [128, 5, 24], F32)
            with self.nc.allow_non_contiguous_dma(reason="tiny per-partition vector loads"):
                kb.dma("sp", vb[:, 0, :], self.hy_b_in.ap()[0].rearrange("(c p) -> p c", p=128), w=[vb], nowaw=True)
                for k in range(3):
                    kb.dma("sp", vb[:, 1 + k, :], self.hy_conv_w.ap()[0][k].rearrange("(c p) -> p c", p=128), w=[vb], nowaw=True)
                kb.dma("sp", vb[:, 4, :], self.hy_conv_b.ap()[0].rearrange("(c p) -> p c", p=128), w=[vb], nowaw=True)
            xin_t = xin.rearrange("(n p) d -> n p d", p=128)
            A1 = self.A[1][0]
            S1 = self.mT[1]
            kb.dma("sp", xt[0][:], xin_t[0], w=[xt[0]])
            for n in range(NT):
                if n + 1 < NT:
                    kb.dma("sp", xt[(n + 1) % 2][:], xin_t[n + 1], w=[xt[(n + 1) % 2]])
                b = xt[n % 2]
                self.norm_T_a1(b[:], [b], ws[n % 2])
                self.norm_T_a2(ws[n % 2])
                if n >= 1:
                    self.norm_T_b(lambda c, n=n: hT[:, c, (n - 1) * 128:n * 128], hk[(n - 1) // 4], A1, S1[:, 0:8],
                                  [A1, S1], ws[(n - 1) % 2])
            self.norm_T_b(lambda c: hT[:, c, (NT - 1) * 128:NT * 128], hk[(NT - 1) // 4], A1, S1[:, 0:8], [A1, S1],
                          ws[(NT - 1) % 2])
            wj = [kb.sbuf(st, "wj%d" % i, [128, 8, 384], BF16) for i in range(1)]
            Z = [[kb.sbuf(st, "Z%d_%d" % (pt, i), [128, 514], F32) for i in range(3)] for pt in range(3)]
            cc = [[kb.sbuf(st, "cc%d_%d" % (pt, i), [128, 512], F32) for i in range(2)] for pt in range(3)]
            u32 = [kb.sbuf(st, "u32_%d" % i, [128, 512], F32) for i in range(2)]
            ubf = [kb.sbuf(st, "ubf_%d" % i, [128, 512], BF16) for i in range(2)]
            pz = [[kb.psum(st, "pz%d_%d" % (pt, i), [128, 512], F32) for i in range(2)] for pt in range(3)]
            w_in_v = self.hy_w_in.ap()[0].rearrange("(kc p) n -> p kc n", p=128)
            for j in range(8):
                wj_ = wj[0]
                for pt in range(3):
                    c0 = pt * 1024 + j * 128
                    kb.dma("pool", wj_[:, :, pt * 128:(pt + 1) * 128], w_in_v[:, :, c0:c0 + 128], w=[wj_], nowaw=(pt > 0))

                def conv(tq, j=j):
                    for pt in range(3):
                        z = Z[pt][tq % 3]
                        c_ = cc[pt][tq % 2]
                        idx = pt * 8 + j
                        kb.op("act", lambda h: h.activation(out=c_[:], in_=z[:, 1:513], func=AF.Identity,
                                                            scale=vb[:, 2, idx:idx + 1], bias=vb[:, 4, idx:idx + 1]),
                              r=[z, vb], w=[c_])
                        kb.op("dve", lambda h: h.scalar_tensor_tensor(out=c_[:], in0=z[:, 0:512], scalar=vb[:, 1, idx:idx + 1],
                                                                      in1=c_[:], op0=ALU.mult, op1=ALU.add),
                              r=[z, vb, c_], w=[c_])
                        kb.op("dve", lambda h: h.scalar_tensor_tensor(out=c_[:], in0=z[:, 2:514], scalar=vb[:, 3, idx:idx + 1],
                                                                      in1=c_[:], op0=ALU.mult, op1=ALU.add),
                              r=[z, vb, c_], w=[c_])
                    u_ = u32[tq % 2]
                    ub = ubf[tq % 2]
                    kb.op("pool", lambda h: h.tensor_tensor(out=u_[:], in0=cc[2][tq % 2][:], in1=cc[1][tq % 2][:], op=ALU.mult),
                          r=[cc[2][tq % 2], cc[1][tq % 2]], w=[u_])
                    kb.op("pool", lambda h: h.tensor_copy(out=ub[:], in_=u_[:]), r=[u_], w=[ub])
                    rows = slice(j * 128, (j + 1) * 128)
                    cols = slice(tq * 512, (tq + 1) * 512)
                    kb.dma("sp", self.u32_v[rows, cols], u_[:], r=[u_])
                    kb.dma("sp", self.ubf_v[rows, cols], ub[:], r=[ub])
                    kb.dma("sp", self.x0_v[rows, cols], cc[0][tq % 2][:], r=[cc[0][tq % 2]])

                for tt in range(16):
                    for pt in range(3):
                        ps = pz[pt][tt % 2]
                        for kc in range(8):
                            kb.op("pe", lambda h, kc=kc: h.matmul(ps[:], lhsT=wj_[:, kc, pt * 128:(pt + 1) * 128],
                                                                  rhs=hT[:, kc, tt * 512:(tt + 1) * 512],
                                                                  start=(kc == 0), stop=(kc == 7)), r=[wj_, hk[tt]], w=[ps])
                        z = Z[pt][tt % 3]
                        idx = pt * 8 + j
                        kb.op("act", lambda h: h.activation(out=z[:, 1:513], in_=ps[:], func=AF.Identity,
                                                            bias=vb[:, 0, idx:idx + 1], scale=1.0), r=[ps, vb], w=[z])
                        if tt == 0:
                            kb.op("pool", lambda h: h.memset(z[:, 0:1], 0.0), w=[z])
                        else:
                            zp = Z[pt][(tt - 1) % 3]
                            kb.op("pool", lambda h: h.tensor_copy(out=z[:, 0:1], in_=zp[:, 512:513]), r=[zp], w=[z])
                            kb.op("pool", lambda h: h.tensor_copy(out=zp[:, 513:514], in_=z[:, 1:2]), r=[z], w=[zp])
                        if tt == 15:
                            kb.op("pool", lambda h: h.memset(z[:, 513:514], 0.0), w=[z])
                    if tt >= 1:
                        conv(tt - 1)
                conv(15)

    def ph_hy_fft(self):
        kb = self.kb
        nc = self.nc
        PI = math.pi
        hd_d = self.scratch("hd_d", [128, NFFT], BF16)
        with kb.phase() as st:
            w1d = kb.sbuf(st, "w1d", [33, 128], F32)
            w2bd = kb.sbuf(st, "w2bd", [128, 128], F32)
            fv = kb.sbuf(st, "fv", [128, 6], F32)
            kb.op("dve", lambda h: h.memset(w2bd[:], 0.0), w=[w2bd])
            with nc.allow_non_contiguous_dma(reason="tiny loads"):
                for hlf in range(2):
                    rs_ = slice(hlf * 64, (hlf + 1) * 64)
                    kb.dma("sp", w1d[:, rs_], self.hy_f_w1.ap()[0][:, :], w=[w1d], nowaw=True)
                    kb.dma("sp", w2bd[rs_, rs_], self.hy_f_w2.ap()[0][:, :], w=[w2bd], nowaw=(hlf > 0))
                    for i, src in enumerate((self.hy_f_b1, self.hy_f_freq1, self.hy_f_b2, self.hy_f_freq2)):
                        kb.dma("sp", fv[rs_, i:i + 1], src.ap()[0].rearrange("(p o) -> p o", o=1), w=[fv], nowaw=True)
            kb.op("dve", lambda h: h.tensor_tensor(out=fv[:, 4:5], in0=fv[:, 0:1], in1=fv[:, 1:2], op=ALU.mult), r=[fv], w=[fv])
            kb.op("dve", lambda h: h.tensor_tensor(out=fv[:, 5:6], in0=fv[:, 2:3], in1=fv[:, 3:4], op=ALU.mult), r=[fv], w=[fv])
            kb.op("dve", lambda h: h.tensor_scalar(out=fv[:], in0=fv[:], scalar1=1.0 / (2.0 * PI), scalar2=None, op0=ALU.mult), r=[fv], w=[fv])
            zt = [kb.sbuf(st, "zt%d" % i, [33, 512], F32) for i in range(2)]
            a1 = [kb.sbuf(st, "a1_%d" % i, [128, 512], F32) for i in range(2)]
            h1 = [kb.sbuf(st, "h1_%d" % i, [128, 512], F32) for i in range(2)]
            a2 = [kb.sbuf(st, "a2_%d" % i, [128, 512], F32) for i in range(2)]
            hdc = [kb.sbuf(st, "hdc%d" % i, [128, 512], BF16) for i in range(2)]
            p1 = [kb.psum(st, "p1_%d" % i, [128, 512], F32) for i in range(2)]
            p2 = [kb.psum(st, "p2_%d" % i, [128, 512], F32) for i in range(2)]
            for pc in range(32):
                i = pc % 2
                cols = slice(pc * 512, (pc + 1) * 512)
                kb.dma("sp", zt[i][:], self.hy_zt.ap()[:, cols], w=[zt[i]])
                kb.op("pe", lambda h: h.matmul(p1[i][:], lhsT=w1d[:], rhs=zt[i][:], start=True, stop=True), r=[w1d, zt[i]], w=[p1[i]])
                kb.op("act", lambda h: h.activation(out=a1[i][:], in_=p1[i][:], func=AF.Identity, scale=fv[:, 1:2], bias=fv[:, 4:5]),
                      r=[p1[i], fv], w=[a1[i]])
                for _rep in range(2):
                    kb.op("dve", lambda h: h.scalar_tensor_tensor(out=a1[i][:], in0=a1[i][:], scalar=0.5, in1=a1[i][:],
                                                                  op0=ALU.is_gt, op1=ALU.subtract), r=[a1[i]], w=[a1[i]])
                kb.op("act", lambda h: h.activation(out=h1[i][:], in_=a1[i][:], func=AF.Sin, scale=2.0 * PI), r=[a1[i]], w=[h1[i]])
                kb.op("pe", lambda h: h.matmul(p2[i][:], lhsT=w2bd[:], rhs=h1[i][:], start=True, stop=True), r=[w2bd, h1[i]], w=[p2[i]])
                kb.op("act", lambda h: h.activation(out=a2[i][:], in_=p2[i][:], func=AF.Identity, scale=fv[:, 3:4], bias=fv[:, 5:6]),
                      r=[p2[i], fv], w=[a2[i]])
                for _rep in range(2):
                    kb.op("dve", lambda h: h.scalar_tensor_tensor(out=a2[i][:], in0=a2[i][:], scalar=0.5, in1=a2[i][:],
                                                                  op0=ALU.is_gt, op1=ALU.subtract), r=[a2[i]], w=[a2[i]])
                kb.op("act", lambda h: h.activation(out=hdc[i][:], in_=a2[i][:], func=AF.Sin, scale=2.0 * PI), r=[a2[i]], w=[hdc[i]])
                if pc < 16:
                    kb.op("pool", lambda h: h.memset(hdc[i][64:128, :], 0.0), w=[hdc[i]])
                else:
                    kb.op("pool", lambda h: h.memset(hdc[i][0:64, :], 0.0), w=[hdc[i]])
                    if pc == 16:
                        kb.op("pool", lambda h: h.memset(hdc[i][64:128, 0:1], 0.0), w=[hdc[i]])
                kb.dma("sp", hd_d.ap()[:, cols], hdc[i][:], r=[hdc[i]])
            if "dbg_hd" in self.taps:
                pass
        with kb.phase() as st:
            f1 = kb.sbuf(st, "f1", [128, 128], BF16)
            r1 = kb.sbuf(st, "r1", [128, 256], BF16)
            r2 = kb.sbuf(st, "r2", [128, 256], BF16)
            tpos = kb.sbuf(st, "tpos", [128, 128], F32)
            negdec = kb.sbuf(st, "negdec", [128, D], F32)
            woutS = kb.sbuf(st, "woutS", [128, D], BF16)
            skipT = kb.sbuf(st, "skipT", [128, 8], F32)
            onesf = kb.sbuf(st, "onesf", [128, 1], F32)
            kb.dma("sp", f1[:], self.fft_f1.ap()[:, :], w=[f1])
            kb.dma("sp", r1[:], self.fft_r1.ap()[:, :], w=[r1])
            kb.dma("sp", r2[:], self.fft_r2.ap()[:, :], w=[r2])
            kb.dma("sp", tpos[:], self.hy_tpos.ap()[:, :], w=[tpos])
            kb.dma("sp", negdec[:], self.hy_decay.ap()[0].partition_broadcast(128), w=[negdec])
            kb.op("act", lambda h: h.activation(out=negdec[:], in_=negdec[:], func=AF.Abs), r=[negdec], w=[negdec])
            kb.op("act", lambda h: h.mul(out=negdec[:], in_=negdec[:], mul=-1.0), r=[negdec], w=[negdec])
            kb.dma("pool", woutS[0:64, :], self.hy_f_wout.ap()[0][:, 0:D], w=[woutS])
            kb.dma("pool", woutS[64:128, :], self.hy_f_wout.ap()[0][:, D:2 * D], w=[woutS], nowaw=True)
            kb.op("act", lambda h: h.mul(out=woutS[64:128, :], in_=woutS[64:128, :], mul=-1.0), r=[woutS], w=[woutS])
            with nc.allow_non_contiguous_dma(reason="tiny loads"):
                kb.dma("sp", skipT[:], self.hy_skip.ap()[0].rearrange("(c p) -> p c", p=128), w=[skipT])
            kb.op("dve", lambda h: h.memset(onesf[:], 1.0), w=[onesf])

            RA = kb.sbuf(st, "RA", [128, 32768], BF16)
            RB = kb.sbuf(st, "RB", [128, 8192], F32)
            RC = kb.sbuf(st, "RC", [128, 16384], BF16)
            kU = kb.key("kU")
            kAT = kb.key("kAT")
            Uv = RA[:, 0:16384].rearrange("p (d b) -> p d b", b=128)
            ATv = RA[:, 16384:32768].rearrange("p (r k d) -> p r k d", r=2, k=64)
            Ev = RA[0:64, :].rearrange("p (r b d) -> p r b d", r=2, b=128)
            Kfv = RB[:].bitcast(BF16).rearrange("p (k r d) -> p k r d", k=64, r=2)
            yT = RB
            YTv = RC[:].rearrange("p (r d k) -> p r d k", r=2, d=128)
            Hdv = RC[:].rearrange("p (a b) -> p b a", b=128)
            PB = [kb.psum(st, "pb%d" % i, [128, 512], F32) for i in range(8)]
            self._pbi = 0

            def bank():
                self._pbi = (self._pbi + 1) % 8
                return PB[self._pbi]

            h2r = [kb.sbuf(st, "h2r%d" % i, [128, 3, 128], BF16) for i in range(6)]
            h4r = [kb.sbuf(st, "h4r%d" % i, [64, 16, 2, 64], BF16) for i in range(2)]
            warg = [kb.sbuf(st, "warg%d" % i, [128, 512], F32) for i in range(2)]
            win = [kb.sbuf(st, "win%d" % i, [128, 512], F32) for i in range(2)]
            ksum = kb.sbuf(st, "ksum", [128, 128], F32)
            rnorm = kb.sbuf(st, "rnorm", [128, 1], F32)
            tA = [kb.sbuf(st, "tA%d" % i, [128, 512], F32) for i in range(2)]
            tB = [kb.sbuf(st, "tB%d" % i, [128, 512], F32) for i in range(2)]
            uc = [kb.sbuf(st, "uc%d" % i, [128, 1024], F32) for i in range(2)]
            xc = [kb.sbuf(st, "xc%d" % i, [128, 1024], F32) for i in range(2)]
            vo = [kb.sbuf(st, "vo%d" % i, [128, 1024], BF16) for i in range(2)]
            self._ev = 0

            def evac(out_ap, in_ap, r, w):
                self._ev += 1
                if self._ev % 2 == 0:
                    kb.op("act", lambda h: h.copy(out=out_ap, in_=in_ap), r=r, w=w)
                else:
                    kb.op("dve", lambda h: h.tensor_copy(out=out_ap, in_=in_ap), r=r, w=w)

            h2cnt = [0]

            def s1(src_v, kparts, src_key):
                for dg in range(32):
                    ps = bank()
                    for dd in range(4):
                        d = dg * 4 + dd
                        kb.op("pe", lambda h: h.matmul(ps[:, dd * 128:(dd + 1) * 128], lhsT=src_v[0:kparts, d, :],
                                                       rhs=f1[0:kparts, :], start=True, stop=True), r=[src_key, f1], w=[ps])
                    evac(ATv[:, :, :, dg * 4:dg * 4 + 4], ps[:].rearrange("p (dd r k) -> p r k dd", dd=4, r=2), [ps], [kAT])

            def s2(consume):
                for kp in range(32):
                    ps = bank()
                    for kk in range(2):
                        k1 = 2 * kp + kk
                        h2 = h2r[h2cnt[0] % 6]
                        h2cnt[0] += 1
                        kb.dma("sp", h2[:], self.fft_h2.ap()[k1], w=[h2])
                        o_r = ps[:, (kk * 2) * 128:(kk * 2 + 1) * 128]
                        o_i = ps[:, (kk * 2 + 1) * 128:(kk * 2 + 2) * 128]
                        kb.op("pe", lambda h: h.matmul(o_r, lhsT=h2[:, 0, :], rhs=ATv[:, 0, k1, :], start=True, stop=False), r=[h2, kAT], w=[ps])
                        kb.op("pe", lambda h: h.matmul(o_r, lhsT=h2[:, 2, :], rhs=ATv[:, 1, k1, :], start=False, stop=True), r=[h2, kAT], w=[ps])
                        kb.op("pe", lambda h: h.matmul(o_i, lhsT=h2[:, 1, :], rhs=ATv[:, 0, k1, :], start=True, stop=False), r=[h2, kAT], w=[ps])
                        kb.op("pe", lambda h: h.matmul(o_i, lhsT=h2[:, 0, :], rhs=ATv[:, 1, k1, :], start=False, stop=True), r=[h2, kAT], w=[ps])
                    consume(kp, ps)

            for db in range(8):
                c0 = db * 128
                for q in range(4):
                    kb.dma("sp", RC[:, q * 4096:(q + 1) * 4096], hd_d.ap()[:, q * 4096:(q + 1) * 4096], w=[RC], nowaw=(q > 0))
                for bg in range(32):
                    ps = bank()
                    b0 = bg * 4
                    for bb in range(4):
                        kb.op("pe", lambda h: h.matmul(ps[:, bb * 128:(bb + 1) * 128], lhsT=Hdv[:, b0 + bb, :],
                                                       rhs=woutS[:, c0:c0 + 128], start=True, stop=True), r=[RC, woutS], w=[ps])
                    wa = warg[bg % 2]
                    wi = win[bg % 2]
                    kb.op("dve", lambda h: h.tensor_tensor(out=wa[:].rearrange("p (b d) -> p b d", d=128),
                                                           in0=tpos[:, b0:b0 + 4].unsqueeze(2).to_broadcast([128, 4, 128]),
                                                           in1=negdec[:, c0:c0 + 128].unsqueeze(1).to_broadcast([128, 4, 128]),
                                                           op=ALU.mult), r=[tpos, negdec], w=[wa])
                    kb.op("act", lambda h: h.activation(out=wi[:], in_=wa[:], func=AF.Exp), r=[wa], w=[wi])
                    kb.op("dve", lambda h: h.tensor_tensor(out=Uv[:, :, b0:b0 + 4].transpose([0, 2, 1]),
                                                           in0=ps[:].rearrange("p (b d) -> p b d", d=128),
                                                           in1=wi[:].rearrange("p (b d) -> p b d", d=128), op=ALU.mult),
                          r=[ps, wi], w=[kU])
                kb.op("dve", lambda h: h.tensor_reduce(out=ksum[:], in_=Uv, axis=AX.X, op=ALU.add, apply_absolute_value=True),
                      r=[kU], w=[ksum])
                pn = bank()
                kb.op("pe", lambda h: h.matmul(pn[:, 0:1], lhsT=ksum[:], rhs=onesf[:], start=True, stop=True), r=[ksum, onesf], w=[pn])
                kb.op("dve", lambda h: h.reciprocal(out=rnorm[:], in_=pn[:, 0:1]), r=[pn], w=[rnorm])
                s1(Uv, 128, kU)
                s2(lambda kp, ps: evac(Kfv[:, 2 * kp:2 * kp + 2, :, :], ps[:].rearrange("p (k r d) -> p k r d", k=2, r=2), [ps], [RB]))
                for q in range(4):
                    kb.dma("sp", Uv[0:64, q * 32:(q + 1) * 32, :],
                           self.ubf_v[c0 + q * 32:c0 + (q + 1) * 32, :].rearrange("d (a b) -> a d b", b=128),
                           w=[kU], nowaw=(q > 0))
                s1(Uv, 64, kU)

                def product(kp, ps):
                    X = ps[:].rearrange("p (k r d) -> p k r d", k=2, r=2)
                    Kk = Kfv[:, 2 * kp:2 * kp + 2, :, :]
                    ta = tA[kp % 2]
                    tb = tB[kp % 2]
                    ta4 = ta[:].rearrange("p (k r d) -> p k r d", k=2, r=2)
                    tb4 = tb[:].rearrange("p (k r d) -> p k r d", k=2, r=2)
                    kb.op("dve", lambda h: h.tensor_tensor(out=ta4, in0=X, in1=Kk, op=ALU.mult), r=[ps, RB], w=[ta])
                    kb.op("dve", lambda h: h.tensor_tensor(out=tb4[:, :, 0, :], in0=X[:, :, 0, :], in1=Kk[:, :, 1, :], op=ALU.mult),
                          r=[ps, RB], w=[tb])
                    kb.op("dve", lambda h: h.tensor_tensor(out=tb4[:, :, 1, :], in0=X[:, :, 1, :], in1=Kk[:, :, 0, :], op=ALU.mult),
                          r=[ps, RB], w=[tb])
                    kb.op("pool", lambda h: h.tensor_tensor(out=YTv[:, 0, :, 2 * kp:2 * kp + 2].transpose([0, 2, 1]),
                                                            in0=ta4[:, :, 0, :], in1=ta4[:, :, 1, :], op=ALU.subtract),
                          r=[ta], w=[RC])
                    kb.op("pool", lambda h: h.tensor_tensor(out=YTv[:, 1, :, 2 * kp:2 * kp + 2].transpose([0, 2, 1]),
                                                            in0=tb4[:, :, 0, :], in1=tb4[:, :, 1, :], op=ALU.add),
                          r=[tb], w=[RC])

                s2(product)
                for dg in range(64):
                    ps = bank()
                    for dd in range(2):
                        d = dg * 2 + dd
                        o_ = ps[0:64, dd * 256:(dd + 1) * 256]
                        kb.op("pe", lambda h: h.matmul(o_, lhsT=YTv[:, 0, d, :], rhs=r1[:], start=True, stop=False), r=[RC, r1], w=[ps])
                        kb.op("pe", lambda h: h.matmul(o_, lhsT=YTv[:, 1, d, :], rhs=r2[:], start=False, stop=True), r=[RC, r2], w=[ps])
                    evac(Ev[:, :, :, dg * 2:dg * 2 + 2], ps[0:64, :].rearrange("p (dd r b) -> p r b dd", dd=2, r=2), [ps], [kU, kAT])
                for bg in range(16):
                    if bg % 2 == 0:
                        h4 = h4r[(bg // 2) % 2]
                        kb.dma("sp", h4[:], self.fft_h4.ap()[:, bg * 8:bg * 8 + 16, :, :], w=[h4])
                    ps = bank()
                    for bb in range(8):
                        b = bg * 8 + bb
                        o_ = ps[:, bb * 64:(bb + 1) * 64]
                        kb.op("pe", lambda h: h.matmul(o_, lhsT=Ev[:, 0, b, :], rhs=h4[:, b % 16, 0, :], start=True, stop=False),
                              r=[kU, kAT, h4], w=[ps])
                        kb.op("pe", lambda h: h.matmul(o_, lhsT=Ev[:, 1, b, :], rhs=h4[:, b % 16, 1, :], start=False, stop=True),
                              r=[kU, kAT, h4], w=[ps])
                    evac(yT[:].rearrange("p (a b) -> p a b", b=128)[:, :, bg * 8:bg * 8 + 8],
                         ps[:].rearrange("p (bb a) -> p a bb", bb=8), [ps], [RB])
                for ck in range(8):
                    cols = slice(ck * 1024, (ck + 1) * 1024)
                    u_ = uc[ck % 2]
                    x_ = xc[ck % 2]
                    v_ = vo[ck % 2]
                    kb.dma("sp", u_[:], self.u32_v[c0:c0 + 128, cols], w=[u_])
                    kb.dma("sp", x_[:], self.x0_v[c0:c0 + 128, cols], w=[x_])
                    kb.op("dve", lambda h: h.tensor_scalar(out=u_[:], in0=u_[:], scalar1=skipT[:, db:db + 1], scalar2=None, op0=ALU.mult),
                          r=[u_, skipT], w=[u_])
                    kb.op("dve", lambda h: h.scalar_tensor_tensor(out=u_[:], in0=yT[:, cols], scalar=rnorm[:, 0:1], in1=u_[:],
                                                                  op0=ALU.mult, op1=ALU.add), r=[RB, rnorm, u_], w=[u_])
                    kb.op("pool", lambda h: h.tensor_tensor(out=v_[:], in0=u_[:], in1=x_[:], op=ALU.mult), r=[u_, x_], w=[v_])
                    kb.dma("sp", self.vT_v[c0:c0 + 128, cols], v_[:], r=[v_])

    def ph_hy_out(self, xin, xout):
        kb = self.kb
        with kb.phase() as st:
            wob = kb.sbuf(st, "wob", [128, 8, D], BF16)
            kb.dma("pool", wob[:], self.hy_w_out.ap()[0].rearrange("(kc p) n -> p kc n", p=128), w=[wob])
            boutb = kb.sbuf(st, "boutb", [128, D], F32)
            kb.dma("sp", boutb[:], self.hy_b_out.ap()[0].partition_broadcast(128), w=[boutb])
            g1b = kb.sbuf(st, "g1b", [128, D], F32)
            kb.dma("sp", g1b[:], self.gb_d[1][0].ap()[:, :], w=[g1b])
            vt = [kb.sbuf(st, "vt%d" % i, [128, 8, 512], BF16) for i in range(2)]
            xt = [kb.sbuf(st, "xt%d" % i, [128, D], F32) for i in range(4)]
            ot = [kb.sbuf(st, "ot%d" % i, [128, 512], F32) for i in range(3)]
            po = [kb.psum(st, "po%d" % i, [128, 512], F32) for i in range(4)]
            xin_t = xin.rearrange("(n p) d -> n p d", p=128)
            xout_t = xout.rearrange("(n p) d -> n p d", p=128)
            vsrc = self.vT_v.rearrange("(kc p) t -> p kc t", p=128)
            for tt in range(16):
                v_ = vt[tt % 2]
                kb.dma("sp", v_[:], vsrc[:, :, tt * 512:(tt + 1) * 512], w=[v_])
                for s in range(4):
                    n = tt * 4 + s
                    b = xt[n % 4]
                    kb.dma("sp", b[:], xin_t[n], w=[b])
                    for hf in range(2):
                        cs = slice(hf * 512, (hf + 1) * 512)
                        p = po[(n * 2 + hf) % 4]
                        for kc in range(8):
                            kb.op("pe", lambda h, kc=kc: h.matmul(p[:], lhsT=v_[:, kc, s * 128:(s + 1) * 128], rhs=wob[:, kc, cs],
                                                                  start=(kc == 0), stop=(kc == 7)), r=[v_, wob], w=[p])
                        o_ = ot[(n * 2 + hf) % 3]
                        kb.op("dve", lambda h: h.tensor_tensor(out=o_[:], in0=p[:], in1=boutb[:, cs], op=ALU.add), r=[p, boutb], w=[o_])
                        kb.op("dve", lambda h: h.tensor_tensor(out=o_[:], in0=o_[:], in1=g1b[:, cs], op=ALU.mult), r=[o_, g1b], w=[o_])
                        kb.op("pool", lambda h: h.tensor_tensor(out=o_[:], in0=o_[:], in1=b[:, cs], op=ALU.add), r=[o_, b], w=[o_])
                        kb.dma("sp", xout_t[n][:, cs], o_[:], r=[o_])

    def norm_T_a1(self, xt, xkeys, ws):
        kb = self.kb
        ss, rs, rstd, junk, xs = ws["ss"], ws["rs"], ws["rstd"], ws["junk"], ws["xs"]
        kb.op("act", lambda h: h.activation(out=junk[:], in_=xt, func=AF.Square, accum_out=ss[:, 0:1]),
              r=list(xkeys), w=[junk, ss])
        kb.op("act", lambda h: h.activation(out=rs[:, 0:1], in_=ss[:, 0:1], func=AF.Sqrt, bias=EPS, scale=1.0 / D),
              r=[ss], w=[rs])
        kb.op("dve", lambda h: h.reciprocal(out=rstd[:, 0:1], in_=rs[:, 0:1]), r=[rs], w=[rstd])
        kb.op("dve", lambda h: h.tensor_scalar(out=xs[:], in0=xt, scalar1=rstd[:, 0:1], scalar2=None, op0=ALU.mult),
              r=[rstd] + list(xkeys), w=[xs])

    def norm_T_a2(self, ws):
        kb = self.kb
        xs = ws["xs"]
        pT, pTk = ws["pT_ap"], ws["pT_key"]
        for c in range(8):
            kb.op("pe", lambda h, c=c: h.transpose(pT[:, c, :], xs[:, c * 128:(c + 1) * 128], self.idb[:]),
                  r=[xs, self.idb], w=[pTk])

    def norm_T_b(self, hT_ap_fn, hkey, A, S, AS_keys, ws):
        kb = self.kb
        pT, pTk = ws["pT_ap"], ws["pT_key"]
        for c in range(8):
            kb.op("act", lambda h, c=c: h.activation(out=hT_ap_fn(c), in_=pT[:, c, :], func=AF.Identity,
                                                     scale=A[:, c:c + 1], bias=S[:, c:c + 1]),
                  r=[pTk] + list(AS_keys), w=[hkey])

    def norm_T(self, xt, xkeys, hT_ap_fn, hkey, A, S, AS_keys, ws):
        self.norm_T_a1(xt, xkeys, ws)
        self.norm_T_a2(ws)
        self.norm_T_b(hT_ap_fn, hkey, A, S, AS_keys, ws)

    def ph_ffn(self, li, xin, xout):
        kb = self.kb
        nc = self.nc
        TT = 256
        NS = TT // 128
        with kb.phase() as st:
            w1b = kb.sbuf(st, "w1b", [128, 8, DFF], BF16)
            w3b = kb.sbuf(st, "w3b", [128, 8, DFF], BF16)
            w2b = kb.sbuf(st, "w2b", [128, NJ, D], BF16)
            kb.dma("pool", w1b[:], self.ffn_w1[li].ap().rearrange("(kc p) n -> p kc n", p=128), w=[w1b])
            kb.dma("pool", w3b[:], self.ffn_w3[li].ap().rearrange("(kc p) n -> p kc n", p=128), w=[w3b])
            kb.dma("pool", w2b[:], self.ffn_w2[li].ap().rearrange("(j p) n -> p j n", p=128), w=[w2b])
            NXB = 4
            xt = [kb.sbuf(st, "xt%d" % i, [128, D], F32) for i in range(NXB)]
            hT = [kb.sbuf(st, "hT%d" % i, [128, 8, TT], BF16) for i in range(2)]
            act = kb.sbuf(st, "act", [128, NJ, TT], BF16)
            ws = []
            for i in range(2):
                ws.append(dict(ss=kb.sbuf(st, "ss%d" % i, [128, 1], F32), rs=kb.sbuf(st, "rs%d" % i, [128, 1], F32),
                               rstd=kb.sbuf(st, "rstd%d" % i, [128, 1], F32),
                               junk=None,
                               xs=kb.sbuf(st, "xs%d" % i, [128, D], BF16),
                               pT_key=kb.psum(st, "pT%d" % i, [128, 8, 128], BF16)))
                ws[-1]["pT_ap"] = ws[-1]["pT_key"][:]
            sil = [kb.sbuf(st, "sil%d" % i, [128, TT], F32) for i in range(2)]
            ot = [kb.sbuf(st, "ot%d" % i, [128, 512], F32) for i in range(3)]
            junk1 = kb.sbuf(st, "junk", [128, D], BF16)
            for w_ in ws:
                w_["junk"] = junk1
            pa = [kb.psum(st, "pa%d" % i, [128, 2, TT], F32) for i in range(2)]
            po = [kb.psum(st, "po%d" % i, [128, 512], F32) for i in range(4)]
            A2 = self.A[li][1]
            S2 = self.mT[li]
            g2b = kb.sbuf(st, "g2b", [128, D], F32)
            kb.dma("sp", g2b[:], self.gb_d[li][1].ap()[:, :], w=[g2b])
            xin_t = xin.rearrange("(n p) d -> n p d", p=128)
            xout_t = xout.rearrange("(n p) d -> n p d", p=128)
            ntt = L // TT

            def load(tt):
                for s in range(NS):
                    n = tt * NS + s
                    b = xt[n % NXB]
                    kb.dma("sp", b[:], xin_t[n], w=[b])

            def norm_part(tt, part):
                hh = hT[tt % 2]
                for s in range(NS):
                    n = tt * NS + s
                    b = xt[n % NXB]
                    w_ = ws[n % 2]
                    if part == 0:
                        self.norm_T_a1(b[:], [b], w_)
                    elif part == 1:
                        self.norm_T_a2(w_)
                    else:
                        self.norm_T_b(lambda c, s=s: hh[:, c, s * 128:(s + 1) * 128], hh, A2, S2[:, 24:32], [A2, S2], w_)

            load(0)
            for part in range(3):
                norm_part(0, part)
            for tt in range(ntt):
                if tt + 1 < ntt:
                    load(tt + 1)
                h_ = hT[tt % 2]
                for j in range(NJ):
                    if tt + 1 < ntt and j in (3, 11, 13):
                        norm_part(tt + 1, {3: 0, 11: 1, 13: 2}[j])
                    p = pa[j % 2]
                    for kc in range(8):
                        kb.op("pe", lambda h, kc=kc: h.matmul(p[:, 0, :], lhsT=w1b[:, kc, j * 128:(j + 1) * 128],
                                                              rhs=h_[:, kc, :], start=(kc == 0), stop=(kc == 7)),
                              r=[w1b, h_], w=[p])
                    for kc in range(8):
                        kb.op("pe", lambda h, kc=kc: h.matmul(p[:, 1, :], lhsT=w3b[:, kc, j * 128:(j + 1) * 128],
                                                              rhs=h_[:, kc, :], start=(kc == 0), stop=(kc == 7)),
                              r=[w3b, h_], w=[p])
                    sl = sil[j % 2]
                    kb.op("act", lambda h: h.activation(out=sl[:], in_=p[:, 0, :], func=AF.Silu), r=[p], w=[sl])
                    kb.op("dve", lambda h: h.tensor_tensor(out=act[:, j, :], in0=p[:, 1, :], in1=sl[:], op=ALU.mult),
                          r=[p, sl], w=[act])
                for s in range(NS):
                    n = tt * NS + s
                    b = xt[n % NXB]
                    for hf in range(2):
                        o_ = ot[(n * 2 + hf) % 3]
                        p = po[(s * 2 + hf) % 4]
                        for j in range(NJ):
                            kb.op("pe", lambda h, j=j: h.matmul(p[:], lhsT=act[:, j, s * 128:(s + 1) * 128],
                                                                rhs=w2b[:, j, hf * 512:(hf + 1) * 512],
                                                                start=(j == 0), stop=(j == NJ - 1)),
                                  r=[act, w2b], w=[p])
                        cs = slice(hf * 512, (hf + 1) * 512)
                        kb.op("dve", lambda h: h.tensor_tensor(out=o_[:], in0=p[:], in1=g2b[:, cs], op=ALU.mult),
                              r=[p, g2b], w=[o_])
                        kb.op("pool", lambda h: h.tensor_tensor(out=o_[:], in0=o_[:], in1=b[:, cs], op=ALU.add),
                              r=[o_, b], w=[o_])
                        kb.dma("sp", xout_t[n][:, cs], o_[:], r=[o_])


def build_program(stop_after=None, taps=(), start_at=None, mode="full"):
    p = Prog(stop_after=stop_after, taps=taps, start_at=start_at, mode=mode)
    nc = p.build()
    return p, nc


_STACKED = ("ada_w", "ada_b", "norm1_g", "norm2_g", "ffn_w1_", "ffn_w3_", "ffn_w2_")


def make_in_maps(inputs, p, xs=None):
    hc = host_consts()
    f = lambda a: np.ascontiguousarray(np.asarray(a, dtype=np.float32))
    shared = {}
    for name in p.din:
        if name in ("x", "c", "ctx"):
            continue
        if name in hc:
            shared[name] = hc[name]
            continue
        for base in _STACKED:
            if name.startswith(base) and name[len(base):].isdigit():
                shared[name] = f(inputs[base.rstrip("_")][int(name[len(base):])])
                break
        else:
            shared[name] = f(inputs[name])
    maps = []
    for b in range(NCORES):
        m = dict(shared)
        m["x"] = f(inputs["x"][b]) if xs is None else xs[b]
        m["c"] = f(inputs["c"][b])
        if "ctx" in p.din:
            m["ctx"] = f(inputs["ctx"][b])
        maps.append(m)
    return maps


SPLIT = False


def kernel(**inputs):
    if not SPLIT:
        p, nc = build_program()
        res = run_bass_kernel_spmd(nc, make_in_maps(inputs, p), core_ids=list(range(NCORES)))
        out = np.stack([np.asarray(r["out"]) for r in res.results], axis=0)
        return out.astype(np.float32)
    p0, nc0 = build_program(mode="l0")
    res0 = run_bass_kernel_spmd(nc0, make_in_maps(inputs, p0), core_ids=list(range(NCORES)))
    x1 = [np.ascontiguousarray(np.asarray(r["out"], dtype=np.float32)) for r in res0.results]
    p1, nc1 = build_program(mode="l1")
    res1 = run_bass_kernel_spmd(nc1, make_in_maps(inputs, p1, xs=x1), core_ids=list(range(NCORES)))
    out = np.stack([np.asarray(r["out"]) for r in res1.results], axis=0)
    return out.astype(np.float32)
```

```python
import contextlib
import math

import ml_dtypes
import numpy as np

import concourse.bass as bass
import concourse.mybir as mybir
from concourse.bass_utils import run_bass_kernel_spmd

F32 = mybir.dt.float32
BF16 = mybir.dt.bfloat16
AF = mybir.ActivationFunctionType
ALU = mybir.AluOpType
AX = mybir.AxisListType

D = 1024
L = 8192
LC = 256
DFF = 2816
NH = 16
NKV = 4
HD = 64
EPS = 1e-6
NCORES = 8
NT = L // 128
NJ = DFF // 128
NFFT = 2 * L


class Key:
    def __init__(self, name=""):
        self.name = name
        self.lastw = None
        self.readers = {}
        self.dsem = None


class Buf(Key):
    def __init__(self, name, t):
        super().__init__(name)
        self.t = t

    def __getitem__(self, idx):
        return self.t[idx]


class Eng:
    def __init__(self, name, h, sem):
        self.name = name
        self.h = h
        self.sem = sem
        self.count = 0
        self.waited = {}


class DSem:
    def __init__(self, sem):
        self.sem = sem
        self.count = 0


class KB:
    def __init__(self, nc, n_dsem=84):
        self.nc = nc
        self.g = contextlib.ExitStack()
        self.engs = {}
        for name, h in (("pe", nc.tensor), ("act", nc.scalar), ("dve", nc.vector),
                        ("pool", nc.gpsimd), ("sp", nc.sync)):
            sem = self.g.enter_context(nc.semaphore("c_" + name))
            self.engs[name] = Eng(name, h, sem)
        self.bar_sem = self.g.enter_context(nc.semaphore("bar"))
        self.bar_count = 0
        self.dpool = [DSem(self.g.enter_context(nc.semaphore("d%d" % i))) for i in range(n_dsem)]
        self.dfree = list(self.dpool)
        self.keys = []
        self.uid = 0

    def key(self, name=""):
        k = Key(name)
        self.keys.append(k)
        return k

    def sbuf(self, stack, name, shape, dtype):
        self.uid += 1
        t = stack.enter_context(self.nc.sbuf_tensor("%s_%d" % (name, self.uid), list(shape), dtype))
        b = Buf(name, t)
        self.keys.append(b)
        return b

    def psum(self, stack, name, shape, dtype):
        self.uid += 1
        t = stack.enter_context(self.nc.psum_tensor("%s_%d" % (name, self.uid), list(shape), dtype))
        b = Buf(name, t)
        self.keys.append(b)
        return b

    def _wait(self, E, ev, same_ok):
        kind, src, val = ev
        if kind == "e":
            if src == E.name and (same_ok or E.name == "pe"):
                return
            if E.waited.get(src, 0) >= val:
                return
            E.waited[src] = val
            E.h.wait_ge(self.engs[src].sem, val)
        else:
            if E.waited.get(id(src), 0) >= val:
                return
            E.waited[id(src)] = val
            E.h.wait_ge(src.sem, val)

    def _deps(self, E, r, w, nowaw=False):
        for k in r:
            if k.lastw is not None:
                self._wait(E, k.lastw, False)
        for k in w:
            if k.lastw is not None and not nowaw:
                self._wait(E, k.lastw, True)
            for ev in k.readers.values():
                self._wait(E, ev, True)

    def _mark(self, ev, r, w):
        for k in w:
            k.lastw = ev
            k.readers = {}
        for k in r:
            if k not in w:
                k.readers[(ev[0], ev[1] if ev[0] == "e" else id(ev[1]))] = ev

    def op(self, en, fn, r=(), w=()):
        E = self.engs[en]
        self._deps(E, r, w)
        inst = fn(E.h)
        E.count += 1
        inst.then_inc(E.sem, 1)
        self._mark(("e", en, E.count), r, w)
        return inst

    def dma(self, qn, out, in_, r=(), w=(), nowaw=False, **kw):
        Q = self.engs[qn]
        self._deps(Q, r, w, nowaw=nowaw)
        k = (list(w) + list(r))[0]
        if k.dsem is None:
            k.dsem = self.dfree.pop()
        inst = Q.h.dma_start(out=out, in_=in_, **kw)
        k.dsem.count += 16
        inst.then_inc(k.dsem.sem, 16)
        self._mark(("d", k.dsem, k.dsem.count), r, w)
        return inst

    def barrier(self, final=False):
        sp = self.engs["sp"]
        for E in self.engs.values():
            if E.name != "sp" and E.count > 0:
                sp.h.wait_ge(E.sem, E.count)
        for ds in self.dpool:
            if ds.count > 0:
                sp.h.wait_ge(ds.sem, ds.count)
        if final:
            return
        for E in self.engs.values():
            if E.name != "sp" and E.count > 0:
                sp.h.sem_clear(E.sem)
        for ds in self.dpool:
            if ds.count > 0:
                sp.h.sem_clear(ds.sem)
                ds.count = 0
        if sp.count > 0:
            sp.h.sem_clear(sp.sem)
        self.bar_count += 1
        sp.h.nop().then_inc(self.bar_sem, 1)
        for E in self.engs.values():
            E.count = 0
            E.waited = {}
            if E.name != "sp":
                E.h.wait_ge(self.bar_sem, self.bar_count)
        for k in self.keys:
            k.lastw = None
            k.readers = {}
            if k.dsem is not None:
                k.dsem = None
        self.dfree = list(self.dpool)

    @contextlib.contextmanager
    def phase(self):
        st = contextlib.ExitStack()
        nkeys = len(self.keys)
        with st:
            yield st
            self.barrier()
        del self.keys[nkeys:]


def _bf(a):
    return np.ascontiguousarray(a.astype(np.float32)).astype(ml_dtypes.bfloat16)


_CONST_CACHE = {}


def host_consts():
    if _CONST_CACHE:
        return _CONST_CACHE
    c = {}
    c["ident_bf"] = _bf(np.eye(128))
    c["ident_f"] = np.eye(128, dtype=np.float32)
    c["ones_bf"] = _bf(np.ones((128, 128)))
    rows = L // 64
    row = np.repeat(np.arange(rows, dtype=np.float32), 64)
    col = np.tile(np.arange(64, dtype=np.float32), rows)
    inv = (np.float32(10000.0) ** (-np.arange(16, dtype=np.float32) / np.float32(16))).astype(np.float32)
    ang = np.concatenate([row[:, None] * inv, col[:, None] * inv], axis=-1).astype(np.float32)
    c["rope_cos"] = np.ascontiguousarray(np.cos(ang).astype(np.float32).reshape(NT, 128, 32).transpose(1, 0, 2))
    c["rope_sin"] = np.ascontiguousarray(np.sin(ang).astype(np.float32).reshape(NT, 128, 32).transpose(1, 0, 2))
    jj = np.arange(128)[:, None]
    ii = np.arange(128)[None, :]
    mprev = np.where(jj >= ii, 0.0, -30000.0)
    mnext = np.where(jj <= ii, 0.0, -30000.0)
    mb = np.stack([np.tile(mprev, (1, 4)), np.tile(mnext, (1, 4))], axis=1)
    c["mask_bias"] = _bf(mb)
    n = np.arange(NFFT)
    m = np.where(n <= L, n, NFFT - n).astype(np.float64)
    t = (m / (L - 1)).astype(np.float32)
    w = (2.0 * np.pi * m / L)
    bands = np.linspace(1e-4, 15.0, 16, dtype=np.float32).astype(np.float64)
    zt = np.concatenate([t[None, :].astype(np.float64), np.cos(bands[:, None] * w[None, :]),
                         -np.sin(bands[:, None] * w[None, :])], axis=0)
    c["hy_zt"] = np.ascontiguousarray(zt.astype(np.float32))
    c["hy_tpos"] = np.ascontiguousarray(t.reshape(128, 128))
    a = np.arange(128)[:, None]
    k1 = np.arange(64)[None, :]
    ph = 2 * np.pi * a * (2 * k1 + 1) / 256.0
    c["fft_f1"] = _bf(np.concatenate([np.cos(ph), -np.sin(ph)], axis=1))
    b_ = np.arange(128)[None, :, None]
    kk = (np.arange(64)[:, None, None] + 128 * np.arange(128)[None, None, :])
    ph = 2 * np.pi * b_ * (2 * kk + 1) / (2.0 * NFFT)
    c["fft_h2"] = _bf(np.stack([np.cos(ph), -np.sin(ph), np.sin(ph)], axis=2))
    k2 = np.arange(128)[:, None]
    bb = np.arange(128)[None, :]
    ps_ = 2 * np.pi * bb * k2 / 128.0
    c["fft_r1"] = _bf(np.concatenate([np.cos(ps_), np.sin(ps_)], axis=1))
    c["fft_r2"] = _bf(np.concatenate([-np.sin(ps_), np.cos(ps_)], axis=1))
    nn = (128 * np.arange(64)[None, None, :] + np.arange(128)[None, :, None])
    ch = 2 * np.pi * nn * (2 * np.arange(64)[:, None, None] + 1) / (2.0 * NFFT)
    c["fft_h4"] = _bf(np.stack([np.cos(ch), -np.sin(ch)], axis=2) * (2.0 / NFFT))
    _CONST_CACHE.update(c)
    return c


class Prog:
    def __init__(self, stop_after=None, taps=(), start_at=None, mode="full"):
        self.mode = mode
        self.layers = {"full": [0, 1], "l0": [0], "l1": [1]}[mode]
        if mode == "l1":
            start_at = "hy"
        self.start_at = start_at
        self.stop_after = stop_after
        self.taps = set(taps)
        nc = bass.Bass("TRN2", target_bir_lowering=False)
        self.nc = nc
        self.kb = KB(nc)
        self.din = {}
        self.outs = []

    def ext_in(self, name, shape, dtype=F32):
        t = self.nc.dram_tensor(name, list(shape), dtype, kind="ExternalInput")
        self.din[name] = t
        return t

    def scratch(self, name, shape, dtype=F32):
        kind = "ExternalOutput" if name in self.taps else "Internal"
        t = self.nc.dram_tensor(name, list(shape), dtype, kind=kind)
        if name in self.taps:
            self.outs.append(name)
        return t

    def declare_inputs(self):
        e = self.ext_in
        ly = self.layers
        self.x = e("x", [L, D])
        self.c = e("c", [D])
        self.ada_w = {i: e("ada_w%d" % i, [D, 6 * D]) for i in ly}
        self.ada_b = {i: e("ada_b%d" % i, [6 * D]) for i in ly}
        self.norm1_g = {i: e("norm1_g%d" % i, [D]) for i in ly}
        self.norm2_g = {i: e("norm2_g%d" % i, [D]) for i in ly}
        self.ffn_w1 = {i: e("ffn_w1_%d" % i, [D, DFF]) for i in ly}
        self.ffn_w3 = {i: e("ffn_w3_%d" % i, [D, DFF]) for i in ly}
        self.ffn_w2 = {i: e("ffn_w2_%d" % i, [DFF, D]) for i in ly}
        self.ident_bf = e("ident_bf", [128, 128], BF16)
        self.ident_f = e("ident_f", [128, 128], F32)
        if 0 in ly:
            self.ctx = e("ctx", [LC, D])
            self.c_ctx = e("c_ctx", [D])
            self.ones_bf = e("ones_bf", [128, 128], BF16)
            self.rope_cos = e("rope_cos", [128, NT, 32])
            self.rope_sin = e("rope_sin", [128, NT, 32])
            self.mask_bias = e("mask_bias", [128, 2, 512], BF16)
            self.attn_wqkv = e("attn_wqkv", [1, D, 1536])
            self.attn_wo = e("attn_wo", [1, D, D])
            self.attn_q_gain = e("attn_q_gain", [1, HD])
            self.attn_k_gain = e("attn_k_gain", [1, HD])
            self.attn_sink = e("attn_sink", [1, NH])
        if 1 in ly:
            self.hy_w_in = e("hy_w_in", [1, D, 3 * D])
            self.hy_b_in = e("hy_b_in", [1, 3 * D])
            self.hy_conv_w = e("hy_conv_w", [1, 3, 3 * D])
            self.hy_conv_b = e("hy_conv_b", [1, 3 * D])
            self.hy_f_w1 = e("hy_f_w1", [1, 33, 64])
            self.hy_f_b1 = e("hy_f_b1", [1, 64])
            self.hy_f_freq1 = e("hy_f_freq1", [1, 64])
            self.hy_f_w2 = e("hy_f_w2", [1, 64, 64])
            self.hy_f_b2 = e("hy_f_b2", [1, 64])
            self.hy_f_freq2 = e("hy_f_freq2", [1, 64])
            self.hy_f_wout = e("hy_f_wout", [1, 64, 2 * D])
            self.hy_decay = e("hy_decay", [1, D])
            self.hy_skip = e("hy_skip", [1, D])
            self.hy_w_out = e("hy_w_out", [1, D, D])
            self.hy_b_out = e("hy_b_out", [1, D])
            self.hy_zt = e("hy_zt", [33, NFFT])
            self.hy_tpos = e("hy_tpos", [128, 128])
            self.fft_f1 = e("fft_f1", [128, 128], BF16)
            self.fft_h2 = e("fft_h2", [64, 128, 3, 128], BF16)
            self.fft_r1 = e("fft_r1", [128, 256], BF16)
            self.fft_r2 = e("fft_r2", [128, 256], BF16)
            self.fft_h4 = e("fft_h4", [64, 128, 2, 64], BF16)

    def build(self):
        kb = self.kb
        nc = self.nc
        self.declare_inputs()
        g = kb.g
        self.idb = kb.sbuf(g, "idb", [128, 128], BF16)
        self.idf = kb.sbuf(g, "idf", [128, 128], F32)
        self.mT = [kb.sbuf(g, "mT%d" % i, [128, 48], F32) for i in range(2)]
        self.cmT = kb.sbuf(g, "cmT", [128, 16], F32)
        self.A = [[kb.sbuf(g, "A%d_%d" % (i, j), [128, 8], F32) for j in range(2)] for i in range(2)]
        self.cA = kb.sbuf(g, "cA", [128, 8], F32)
        self.gb_d = [[self.nc.dram_tensor("gb_d%d_%d" % (i, j), [128, D], F32, kind="Internal") for j in range(2)] for i in range(2)]
        kb.dma("sp", self.idb[:], self.ident_bf.ap()[:, :], w=[self.idb])
        kb.dma("sp", self.idf[:], self.ident_f.ap()[:, :], w=[self.idf])

        self.ph_ada()
        if self.stop_after == "ada":
            return self.finish()
        nc_ = self.nc
        SA = nc_.dram_tensor("SA", [L * D], F32, kind="Internal")
        SC = nc_.dram_tensor("SC", [L * D], F32, kind="Internal")
        tok = lambda t: t.ap().rearrange("(l d) -> l d", d=D)
        chn = lambda t: t.ap().rearrange("(d l) -> d l", l=L)
        if self.start_at == "hy":
            x1v = self.x.ap()
        else:
            self.ph_attn(self.x.ap(), tok(SA))
            if self.stop_after == "attn":
                return self.finish()
            if self.mode == "l0":
                x1t = nc_.dram_tensor("out", [L, D], F32, kind="ExternalOutput")
                x1v = x1t.ap()
            else:
                SB = nc_.dram_tensor("SB", [L * D], F32, kind="Internal")
                x1v = tok(SB)
            self.ph_ffn(0, tok(SA), x1v)
            if self.stop_after == "ffn0" or self.mode == "l0":
                return self.finish()
        self.u32_v = chn(SA)
        self.x0_v = chn(SC)
        self.ubf_v = self.scratch("ubf_d", [D, L], BF16).ap()
        self.ph_hy_in(x1v)
        if self.stop_after == "hyin":
            return self.finish()
        self.vT_v = self.scratch("vT_d", [D, L], BF16).ap()
        self.ph_hy_fft()
        if self.stop_after == "hyfft":
            return self.finish()
        self.ph_hy_out(x1v, tok(SA))
        if self.stop_after == "hyout":
            return self.finish()
        self.out = self.nc.dram_tensor("out", [L, D], F32, kind="ExternalOutput")
        self.ph_ffn(getattr(self, "last_li", 1), tok(SA), self.out.ap())
        return self.finish()

    def finish(self):
        self.kb.barrier(final=True)
        self.kb.g.close()
        return self.nc

    def ph_ada(self):
        kb = self.kb
        with kb.phase() as st:
            cT = kb.sbuf(st, "cT", [128, 16], F32)
            sT = kb.sbuf(st, "sT", [128, 16], F32)
            scb = kb.sbuf(st, "scb", [128, 16, 128], F32)
            wbuf = [kb.sbuf(st, "adaw%d" % i, [128, 8, 512], F32) for i in range(2)]
            bb = [kb.sbuf(st, "adab%d" % i, [128, 512], F32) for i in range(2)]
            B6 = kb.sbuf(st, "B6", [128, 6 * D], F32)
            CB = kb.sbuf(st, "CB", [128, 2 * D], F32)
            tmp = kb.sbuf(st, "adatmp", [128, 48, 128], F32)
            ngT = kb.sbuf(st, "ngT", [128, 4, 8], F32)
            one = kb.sbuf(st, "one", [128, 8], F32)
            ps = [kb.psum(st, "adaps%d" % i, [128, 512], F32) for i in range(4)]
            nc = self.nc
            with nc.allow_non_contiguous_dma(reason="tiny per-partition vector loads"):
                kb.dma("sp", cT[:, 0:8], self.c.ap().rearrange("(c p) -> p c", p=128), w=[cT])
                if 0 in self.layers:
                    kb.dma("sp", cT[:, 8:16], self.c_ctx.ap().rearrange("(c p) -> p c", p=128), w=[cT], nowaw=True)
                else:
                    kb.dma("sp", cT[:, 8:16], self.c.ap().rearrange("(c p) -> p c", p=128), w=[cT], nowaw=True)
                for i in self.layers:
                    kb.dma("sp", ngT[:, i, :], self.norm1_g[i].ap().rearrange("(c p) -> p c", p=128), w=[ngT], nowaw=True)
                    kb.dma("sp", ngT[:, 2 + i, :], self.norm2_g[i].ap().rearrange("(c p) -> p c", p=128), w=[ngT], nowaw=True)
            kb.op("act", lambda h: h.activation(out=sT[:], in_=cT[:], func=AF.Silu), r=[cT], w=[sT])
            kb.op("dve", lambda h: h.tensor_copy(out=scb[:], in_=sT[:].unsqueeze(2).to_broadcast([128, 16, 128])),
                  r=[sT], w=[scb])
            cnt = 0
            for i in self.layers:
                for gidx in range(12):
                    wb = wbuf[cnt % 2]
                    b_ = bb[cnt % 2]
                    p = ps[cnt % 2]
                    cols = slice(gidx * 512, (gidx + 1) * 512)
                    kb.dma("sp", wb[:], self.ada_w[i].ap()[:, cols].rearrange("(kc p) n -> p kc n", p=128), w=[wb])
                    kb.dma("sp", b_[:], self.ada_b[i].ap()[cols].partition_broadcast(128), w=[b_])
                    for kc in range(8):
                        kb.op("pe", lambda h, kc=kc: h.matmul(p[:], lhsT=scb[:, kc, :], rhs=wb[:, kc, :],
                                                              start=(kc == 0), stop=(kc == 7)),
                              r=[scb, wb], w=[p])
                    kb.op("dve", lambda h: h.tensor_tensor(out=B6[:, cols], in0=p[:], in1=b_[:], op=ALU.add),
                          r=[p, b_], w=[B6])
                    if i == 0 and gidx < 4:
                        p2 = ps[2 + cnt % 2]
                        for kc in range(8):
                            kb.op("pe", lambda h, kc=kc: h.matmul(p2[:], lhsT=scb[:, 8 + kc, :], rhs=wb[:, kc, :],
                                                                  start=(kc == 0), stop=(kc == 7)),
                                  r=[scb, wb], w=[p2])
                        kb.op("dve", lambda h: h.tensor_tensor(out=CB[:, cols], in0=p2[:], in1=b_[:], op=ALU.add),
                              r=[p2, b_], w=[CB])
                    cnt += 1
                kb.op("dve", lambda h: h.tensor_tensor(out=tmp[:], in0=B6[:].rearrange("p (j n) -> p j n", n=128),
                                                       in1=self.idf[:].unsqueeze(1).to_broadcast([128, 48, 128]),
                                                       op=ALU.mult), r=[B6, self.idf], w=[tmp])
                kb.op("dve", lambda h: h.tensor_reduce(out=self.mT[i][:], in_=tmp[:], axis=AX.X, op=ALU.add),
                      r=[tmp], w=[self.mT[i]])
                kb.dma("sp", self.gb_d[i][0].ap()[:, :], B6[:, 2 * D:3 * D], r=[B6])
                kb.dma("sp", self.gb_d[i][1].ap()[:, :], B6[:, 5 * D:6 * D], r=[B6])
                if i == 0:
                    kb.op("dve", lambda h: h.tensor_tensor(out=tmp[:, 0:16, :], in0=CB[:].rearrange("p (j n) -> p j n", n=128),
                                                           in1=self.idf[:].unsqueeze(1).to_broadcast([128, 16, 128]),
                                                           op=ALU.mult), r=[CB, self.idf], w=[tmp])
                    kb.op("dve", lambda h: h.tensor_reduce(out=self.cmT[:], in_=tmp[:, 0:16, :], axis=AX.X, op=ALU.add),
                          r=[tmp], w=[self.cmT])
                for j in range(2):
                    sc = self.mT[i][:, (8 + 24 * j):(16 + 24 * j)]
                    kb.op("dve", lambda h: h.tensor_scalar(out=one[:], in0=sc, scalar1=1.0, scalar2=None, op0=ALU.add),
                          r=[self.mT[i]], w=[one])
                    kb.op("dve", lambda h: h.tensor_tensor(out=self.A[i][j][:], in0=one[:], in1=ngT[:, 2 * j + i, :], op=ALU.mult),
                          r=[one, ngT], w=[self.A[i][j]])
                if i == 0:
                    kb.op("dve", lambda h: h.tensor_scalar(out=one[:], in0=self.cmT[:, 8:16], scalar1=1.0, scalar2=None, op0=ALU.add),
                          r=[self.cmT], w=[one])
                    kb.op("dve", lambda h: h.tensor_tensor(out=self.cA[:], in0=one[:], in1=ngT[:, 0, :], op=ALU.mult),
                          r=[one, ngT], w=[self.cA])
            if "dbg_ada" in self.taps:
                dbg = self.scratch("dbg_ada", [128, 48 * 2 + 16 + 8 * 5])
                o = 0
                for t_, n in ((self.mT[0], 48), (self.mT[1], 48), (self.cmT, 16), (self.A[0][0], 8), (self.A[0][1], 8),
                              (self.A[1][0], 8), (self.A[1][1], 8), (self.cA, 8)):
                    kb.dma("sp", dbg.ap()[:, o:o + n], t_[:], r=[t_])
                    o += n


    def ph_attn(self, xin, xout):
        kb = self.kb
        nc = self.nc
        with kb.phase() as st:
            wqkv = kb.sbuf(st, "wqkv", [128, 8, 1536], BF16)
            wo = kb.sbuf(st, "wo", [64, NH, D], BF16)
            kb.dma("pool", wqkv[:], self.attn_wqkv.ap()[0].rearrange("(kc p) n -> p kc n", p=128), w=[wqkv])
            kb.dma("pool", wo[:], self.attn_wo.ap()[0].rearrange("(h d) n -> d h n", d=64), w=[wo])
            gq = kb.sbuf(st, "gq", [128, 20, 64], F32)
            for h_ in range(20):
                src = self.attn_q_gain if h_ < 16 else self.attn_k_gain
                kb.dma("sp", gq[:, h_, :], src.ap()[0].partition_broadcast(128), w=[gq], nowaw=True)
            esink = kb.sbuf(st, "esink", [128, NH], F32)
            kb.dma("sp", esink[:], self.attn_sink.ap()[0].partition_broadcast(128), w=[esink])
            kb.op("act", lambda h: h.activation(out=esink[:], in_=esink[:], func=AF.Exp), r=[esink], w=[esink])
            cosT = kb.sbuf(st, "cosT", [128, NT, 32], F32)
            sinT = kb.sbuf(st, "sinT", [128, NT, 32], F32)
            kb.dma("sp", cosT[:], self.rope_cos.ap()[:, :, :], w=[cosT])
            kb.dma("sp", sinT[:], self.rope_sin.ap()[:, :, :], w=[sinT])
            MB = kb.sbuf(st, "MB", [128, 2, 512], BF16)
            kb.dma("sp", MB[:], self.mask_bias.ap()[:, :, :], w=[MB])
            onesb = kb.sbuf(st, "onesb", [128, 128], BF16)
            kb.dma("sp", onesb[:], self.ones_bf.ap()[:, :], w=[onesb])

            NXB = 4
            xt = [kb.sbuf(st, "xt%d" % i, [128, D], F32) for i in range(NXB)]
            hT = [kb.sbuf(st, "hT%d" % i, [128, 8, 128], BF16) for i in range(2)]
            junk1 = kb.sbuf(st, "junk", [128, D], BF16)
            PB = [kb.psum(st, "pb%d" % i, [128, 512], F32) for i in range(8)]
            pT0 = PB[0]
            ws = []
            for i in range(2):
                ws.append(dict(ss=kb.sbuf(st, "ss%d" % i, [128, 1], F32), rs=kb.sbuf(st, "rs%d" % i, [128, 1], F32),
                               rstd=kb.sbuf(st, "rstd%d" % i, [128, 1], F32), junk=junk1,
                               xs=kb.sbuf(st, "xs%d" % i, [128, D], BF16)))
            sq = kb.sbuf(st, "sq", [128, 1280], F32)
            ssq = kb.sbuf(st, "ssq", [128, 20], F32)
            rsq = kb.sbuf(st, "rsq", [128, 20], F32)
            rstq = kb.sbuf(st, "rstq", [128, 20], F32)
            qn = kb.sbuf(st, "qn", [128, 1280], F32)
            t1 = kb.sbuf(st, "t1", [128, 640], F32)
            t2 = kb.sbuf(st, "t2", [128, 640], F32)
            t3 = kb.sbuf(st, "t3", [128, 640], F32)
            t4 = kb.sbuf(st, "t4", [128, 640], F32)
            qr = [kb.sbuf(st, "qr%d" % i, [128, 1280], BF16) for i in range(2)]
            QT = [kb.sbuf(st, "QT%d" % i, [64, NH * 128], BF16) for i in range(3)]
            NKR = 4
            KT = [kb.sbuf(st, "KT%d" % i, [64, NKV * 128], BF16) for i in range(NKR)]
            VV = [kb.sbuf(st, "VV%d" % i, [128, NKV * 64], BF16) for i in range(NKR)]
            KTc = [kb.sbuf(st, "KTc%d" % i, [64, NKV * 128], BF16) for i in range(2)]
            VVc = [kb.sbuf(st, "VVc%d" % i, [128, NKV * 64], BF16) for i in range(2)]
            PT = [kb.sbuf(st, "PT%d" % i, [128, 5 * 512], BF16) for i in range(2)]
            rr = kb.sbuf(st, "rr", [64, 512], F32)
            OT = [kb.sbuf(st, "OT%d" % i, [64, NH * 128], BF16) for i in range(2)]
            ot = [kb.sbuf(st, "ot%d" % i, [128, 512], F32) for i in range(3)]
            xin_t = xin.rearrange("(n p) d -> n p d", p=128)
            xout_t = xout.rearrange("(n p) d -> n p d", p=128)
            ctx_t = self.ctx.ap().rearrange("(n p) d -> n p d", p=128)
            g1b = kb.sbuf(st, "g1b", [128, D], F32)
            kb.dma("sp", g1b[:], self.gb_d[0][0].ap()[:, :], w=[g1b])
            xslot = lambda n: xt[(n + 2) % NXB]

            def bf(pb, shape_str=None, **kw):
                a = pb[:].bitcast(BF16)
                return a

            def stage_a(xtile, xkey, A, S, w_, hbuf, cs_idx, kdst, vdst, qdst, qrb):
                w_["pT_ap"] = bf(pT0).rearrange("p (c t) -> p c t", t=128)
                w_["pT_key"] = pT0
                only_kv = qdst is None
                cgs = (2,) if only_kv else (0, 1, 2)
                h0 = 16 if only_kv else 0
                QB = lambda cg: PB[1 + cg]

                def p1():
                    self.norm_T_a1(xtile, [xkey], w_)

                def p2():
                    self.norm_T_a2(w_)

                def p3():
                    self.norm_T_b(lambda c: hbuf[:, c, :], hbuf, A, S[0], [A, S[1]], w_)

                def p4():
                    for cg in cgs:
                        pq = QB(cg)
                        for kc in range(8):
                            kb.op("pe", lambda h, kc=kc: h.matmul(pq[:], lhsT=hbuf[:, kc, :], rhs=wqkv[:, kc, cg * 512:(cg + 1) * 512],
                                                                  start=(kc == 0), stop=(kc == 7)), r=[hbuf, wqkv], w=[pq])

                def p5():
                    for cg in cgs:
                        pq = QB(cg)
                        n = 512 if cg < 2 else 256
                        kb.op("act", lambda h: h.activation(out=sq[:, cg * 512:cg * 512 + n], in_=pq[:, 0:n], func=AF.Square),
                              r=[pq], w=[sq])
                    kb.op("dve", lambda h: h.tensor_reduce(out=ssq[:, h0:20], in_=sq[:, h0 * 64:1280].rearrange("p (h d) -> p h d", d=64),
                                                           axis=AX.X, op=ALU.add), r=[sq], w=[ssq])
                    kb.op("act", lambda h: h.activation(out=rsq[:, h0:20], in_=ssq[:, h0:20], func=AF.Sqrt, bias=EPS, scale=1.0 / HD),
                          r=[ssq], w=[rsq])
                    kb.op("dve", lambda h: h.reciprocal(out=rstq[:, h0:20], in_=rsq[:, h0:20]), r=[rsq], w=[rstq])
                    for cg in cgs:
                        pq = QB(cg)
                        nh = 8 if cg < 2 else 4
                        hs = cg * 8
                        kb.op("dve", lambda h: h.tensor_tensor(out=qn[:, hs * 64:(hs + nh) * 64].rearrange("p (h d) -> p h d", d=64),
                                                               in0=pq[:, 0:nh * 64].rearrange("p (h d) -> p h d", d=64),
                                                               in1=rstq[:, hs:hs + nh].unsqueeze(2).to_broadcast([128, nh, 64]),
                                                               op=ALU.mult), r=[pq, rstq], w=[qn])
                    kb.op("act", lambda h: h.copy(out=vdst[:], in_=QB(2)[:, 256:512]), r=[QB(2)], w=[vdst])
                    qn3 = qn[:, h0 * 64:1280].rearrange("p (h d) -> p h d", d=64)
                    kb.op("pool", lambda h: h.tensor_tensor(out=qn3, in0=qn3, in1=gq[:, h0:20, :], op=ALU.mult), r=[qn, gq], w=[qn])
                    qrv = qrb[:, h0 * 64:1280]
                    if cs_idx is None:
                        kb.op("dve", lambda h: h.tensor_copy(out=qrv, in_=qn[:, h0 * 64:1280]), r=[qn], w=[qrb])
                    else:
                        q4 = qn[:, :].rearrange("p (h two j) -> p h two j", two=2, j=32)
                        o4 = qrb[:, :].rearrange("p (h two j) -> p h two j", two=2, j=32)
                        x1 = q4[:, :, 0, :]
                        x2 = q4[:, :, 1, :]
                        cb = cosT[:, cs_idx, :].unsqueeze(1).to_broadcast([128, 20, 32])
                        sb_ = sinT[:, cs_idx, :].unsqueeze(1).to_broadcast([128, 20, 32])
                        v3 = lambda t_: t_[:, :].rearrange("p (h j) -> p h j", j=32)
                        kb.op("dve", lambda h: h.tensor_tensor(out=v3(t1), in0=x1, in1=cb, op=ALU.mult), r=[qn, cosT], w=[t1])
                        kb.op("pool", lambda h: h.tensor_tensor(out=v3(t2), in0=x2, in1=sb_, op=ALU.mult), r=[qn, sinT], w=[t2])
                        kb.op("dve", lambda h: h.tensor_tensor(out=v3(t3), in0=x1, in1=sb_, op=ALU.mult), r=[qn, sinT], w=[t3])
                        kb.op("pool", lambda h: h.tensor_tensor(out=v3(t4), in0=x2, in1=cb, op=ALU.mult), r=[qn, cosT], w=[t4])
                        kb.op("dve", lambda h: h.tensor_tensor(out=o4[:, :, 0, :], in0=v3(t1), in1=v3(t2), op=ALU.subtract),
                              r=[t1, t2], w=[qrb])
                        kb.op("pool", lambda h: h.tensor_tensor(out=o4[:, :, 1, :], in0=v3(t3), in1=v3(t4), op=ALU.add),
                              r=[t3, t4], w=[qrb])

                def p6():
                    pb = PB[0]
                    pv = bf(pb)
                    if not only_kv:
                        for half in range(2):
                            for hh in range(8):
                                hd = half * 8 + hh
                                kb.op("pe", lambda h, hd=hd, hh=hh: h.transpose(pv[0:64, hh * 128:(hh + 1) * 128],
                                                                              qrb[:, hd * 64:(hd + 1) * 64], self.idb[:]),
                                      r=[qrb, self.idb], w=[pb])
                            kb.op("act" if half == 0 else "dve",
                                  (lambda h: h.copy(out=qdst[:, 0:1024], in_=pv[0:64, 0:1024])) if half == 0 else
                                  (lambda h: h.tensor_copy(out=qdst[:, 1024:2048], in_=pv[0:64, 0:1024])),
                                  r=[pb], w=[qdst])
                    for hh in range(4):
                        kb.op("pe", lambda h, hh=hh: h.transpose(pv[0:64, hh * 128:(hh + 1) * 128],
                                                                qrb[:, (16 + hh) * 64:(17 + hh) * 64], self.idb[:]),
                              r=[qrb, self.idb], w=[pb])
                    kb.op("act", lambda h: h.copy(out=kdst[:], in_=pv[0:64, 0:512]), r=[pb], w=[kdst])

                return [p1, p2, p3, p4, p5, p6]

            sbanks = [PB[4], PB[5]]
            sb_cnt = [0]

            def chunks_of(t):
                ch = [(KT[t % NKR], VV[t % NKR], None)]
                if t > 0:
                    ch.append((KT[(t - 1) % NKR], VV[(t - 1) % NKR], 0))
                if t < NT - 1:
                    ch.append((KT[(t + 1) % NKR], VV[(t + 1) % NKR], 1))
                ch.append((KTc[0], VVc[0], None))
                ch.append((KTc[1], VVc[1], None))
                return ch

            def scores(t, g):
                qt = QT[t % 3]
                pt = PT[g % 2]
                qcols = qt[:, g * 512:(g + 1) * 512]
                for i, (kt_, v_, m) in enumerate(chunks_of(t)):
                    ps_ = sbanks[sb_cnt[0] % 2]
                    sb_cnt[0] += 1
                    kb.op("pe", lambda h: h.matmul(ps_[:], lhsT=kt_[:, g * 128:(g + 1) * 128], rhs=qcols,
                                                   start=True, stop=(m is None)), r=[kt_, qt], w=[ps_])
                    if m is not None:
                        kb.op("pe", lambda h: h.matmul(ps_[:], lhsT=self.idb[:], rhs=MB[:, m, :], start=False, stop=True),
                              r=[self.idb, MB], w=[ps_])
                    kb.op("act", lambda h: h.activation(out=pt[:, i * 512:(i + 1) * 512], in_=ps_[:], func=AF.Exp,
                                                        scale=float(HD) ** -0.5), r=[ps_], w=[pt])

            def rest(t, g):
                o_t = OT[t % 2]
                pt = PT[g % 2]
                chunks = chunks_of(t)
                nchk = len(chunks)
                pden = PB[6]
                pov = PB[7]
                for i in range(nchk):
                    kb.op("pe", lambda h: h.matmul(pden[:], lhsT=onesb[:], rhs=pt[:, i * 512:(i + 1) * 512],
                                                   start=(i == 0), stop=(i == nchk - 1)), r=[onesb, pt], w=[pden])
                for i, (kt_, v_, m) in enumerate(chunks):
                    kb.op("pe", lambda h: h.matmul(pov[0:64, :], lhsT=v_[:, g * 64:(g + 1) * 64], rhs=pt[:, i * 512:(i + 1) * 512],
                                                   start=(i == 0), stop=(i == nchk - 1)), r=[v_, pt], w=[pov])
                kb.op("dve", lambda h: h.tensor_tensor(out=rr[:, :].rearrange("p (h q) -> p h q", q=128),
                                                       in0=pden[0:64, :].rearrange("p (h q) -> p h q", q=128),
                                                       in1=esink[0:64, 4 * g:4 * g + 4].unsqueeze(2).to_broadcast([64, 4, 128]),
                                                       op=ALU.add), r=[pden, esink], w=[rr])
                kb.op("dve", lambda h: h.reciprocal(out=rr[:], in_=rr[:]), r=[rr], w=[rr])
                kb.op("dve", lambda h: h.tensor_tensor(out=o_t[:, g * 512:(g + 1) * 512], in0=pov[0:64, :], in1=rr[:], op=ALU.mult),
                      r=[pov, rr], w=[o_t])

            def wo_fn(t):
                o_t = OT[t % 2]
                b = xslot(t)
                for hf in range(2):
                    pw = PB[4] if hf == 0 else PB[5]
                    cs = slice(hf * 512, (hf + 1) * 512)
                    for hd in range(NH):
                        kb.op("pe", lambda h, hd=hd: h.matmul(pw[:], lhsT=o_t[:, hd * 128:(hd + 1) * 128], rhs=wo[:, hd, cs],
                                                              start=(hd == 0), stop=(hd == NH - 1)), r=[o_t, wo], w=[pw])
                    o_ = ot[(t * 2 + hf) % 3]
                    kb.op("dve", lambda h: h.tensor_tensor(out=o_[:], in0=pw[:], in1=g1b[:, cs], op=ALU.mult), r=[pw, g1b], w=[o_])
                    kb.op("pool", lambda h: h.tensor_tensor(out=o_[:], in0=o_[:], in1=b[:, cs], op=ALU.add), r=[o_, b], w=[o_])
                    kb.dma("sp", xout_t[t][:, cs], o_[:], r=[o_])

            for ci in range(2):
                b = xt[ci]
                kb.dma("sp", b[:], ctx_t[ci], w=[b])
                for f in stage_a(b[:], b, self.cA, (self.cmT[:, 0:8], self.cmT), ws[ci % 2], hT[ci % 2], None,
                                 KTc[ci], VVc[ci], None, qr[ci % 2]):
                    f()
            kb.dma("sp", xt[2][:], xin_t[0], w=[xt[2]])
            noop = [lambda: None] * 6
            for n in range(NT + 2):
                if n + 1 < NT:
                    kb.dma("sp", xslot(n + 1)[:], xin_t[n + 1], w=[xslot(n + 1)])
                if n < NT:
                    b = xslot(n)
                    A_ = stage_a(b[:], b, self.A[0][0], (self.mT[0][:, 0:8], self.mT[0]), ws[n % 2], hT[n % 2], n,
                                 KT[n % NKR], VV[n % NKR], QT[n % 3], qr[n % 2])
                else:
                    A_ = noop
                tb = n - 2
                has_b = 0 <= tb < NT
                A_[0]()
                if has_b:
                    scores(tb, 0)
                A_[1]()
                A_[2]()
                if has_b:
                    scores(tb, 1)
                    rest(tb, 0)
                A_[3]()
                A_[4]()
                if has_b:
                    scores(tb, 2)
                    rest(tb, 1)
                    scores(tb, 3)
                    rest(tb, 2)
                    rest(tb, 3)
                A_[5]()
                if has_b:
                    wo_fn(tb)

    def ph_hy_in(self, xin):
        kb = self.kb
        with kb.phase() as st:
            hT = kb.sbuf(st, "hTall", [128, 8, L], BF16)
            hk = [kb.key("hk%d" % i) for i in range(16)]
            xt = [kb.sbuf(st, "xt%d" % i, [128, D], F32) for i in range(2)]
            junk1 = kb.sbuf(st, "junk", [128, D], BF16)
            ws = []
            for i in range(2):
                pk = kb.psum(st, "pT%d" % i, [128, 8, 128], BF16)
                ws.append(dict(ss=kb.sbuf(st, "ss%d" % i, [128, 1], F32), rs=kb.sbuf(st, "rs%d" % i, [128, 1], F32),
                               rstd=kb.sbuf(st, "rstd%d" % i, [128, 1], F32), junk=junk1,
                               xs=kb.sbuf(st, "xs%d" % i, [128, D], BF16), pT_key=pk, pT_ap=pk[:]))
            vb = kb.sbuf(st, "hyvec", [128, 5, 24], F32)
            with self.nc.allow_non_contiguous_dma(reason="tiny per-partition vector loads"):
                kb.dma("sp", vb[:, 0, :], self.hy_b_in.ap()[0].rearrange("(c p) -> p c", p=128), w=[vb], nowaw=True)
                for k in range(3):
                    kb.dma("sp", vb[:, 1 + k, :], self.hy_conv_w.ap()[0][k].rearrange("(c p) -> p c", p=128), w=[vb], nowaw=True)
                kb.dma("sp", vb[:, 4, :], self.hy_conv_b.ap()[0].rearrange("(c p) -> p c", p=128), w=[vb], nowaw=True)
            xin_t = xin.rearrange("(n p) d -> n p d", p=128)
            A1 = self.A[1][0]
            S1 = self.mT[1]
            kb.dma("sp", xt[0][:], xin_t[0], w=[xt[0]])
            for n in range(NT):
                if n + 1 < NT:
                    kb.dma("sp", xt[(n + 1) % 2][:], xin_t[n + 1], w=[xt[(n + 1) % 2]])
                b = xt[n % 2]
                self.norm_T_a1(b[:], [b], ws[n % 2])
                self.norm_T_a2(ws[n % 2])
                if n >= 1:
                    self.norm_T_b(lambda c, n=n: hT[:, c, (n - 1) * 128:n * 128], hk[(n - 1) // 4], A1, S1[:, 0:8],
                                  [A1, S1], ws[(n - 1) % 2])
            self.norm_T_b(lambda c: hT[:, c, (NT - 1) * 128:NT * 128], hk[(NT - 1) // 4], A1, S1[:, 0:8], [A1, S1],
                          ws[(NT - 1) % 2])
            wj = [kb.sbuf(st, "wj%d" % i, [128, 8, 384], BF16) for i in range(1)]
            Z = [[kb.sbuf(st, "Z%d_%d" % (pt, i), [128, 514], F32) for i in range(3)] for pt in range(3)]
            cc = [[kb.sbuf(st, "cc%d_%d" % (pt, i), [128, 512], F32) for i in range(2)] for pt in range(3)]
            u32 = [kb.sbuf(st, "u32_%d" % i, [128, 512], F32) for i in range(2)]
            ubf = [kb.sbuf(st, "ubf_%d" % i, [128, 512], BF16) for i in range(2)]
            pz = [[kb.psum(st, "pz%d_%d" % (pt, i), [128, 512], F32) for i in range(2)] for pt in range(3)]
            w_in_v = self.hy_w_in.ap()[0].rearrange("(kc p) n -> p kc n", p=128)
            for j in range(8):
                wj_ = wj[0]
                for pt in range(3):
                    c0 = pt * 1024 + j * 128
                    kb.dma("pool", wj_[:, :, pt * 128:(pt + 1) * 128], w_in_v[:, :, c0:c0 + 128], w=[wj_], nowaw=(pt > 0))

                def conv(tq, j=j):
                    for pt in range(3):
                        z = Z[pt][tq % 3]
                        c_ = cc[pt][tq % 2]
                        idx = pt * 8 + j
                        kb.op("act", lambda h: h.activation(out=c_[:], in_=z[:, 1:513], func=AF.Identity,
                                                            scale=vb[:, 2, idx:idx + 1], bias=vb[:, 4, idx:idx + 1]),
                              r=[z, vb], w=[c_])
                        kb.op("dve", lambda h: h.scalar_tensor_tensor(out=c_[:], in0=z[:, 0:512], scalar=vb[:, 1, idx:idx + 1],
                                                                      in1=c_[:], op0=ALU.mult, op1=ALU.add),
                              r=[z, vb, c_], w=[c_])
                        kb.op("dve", lambda h: h.scalar_tensor_tensor(out=c_[:], in0=z[:, 2:514], scalar=vb[:, 3, idx:idx + 1],
                                                                      in1=c_[:], op0=ALU.mult, op1=ALU.add),
                              r=[z, vb, c_], w=[c_])
                    u_ = u32[tq % 2]
                    ub = ubf[tq % 2]
                    kb.op("pool", lambda h: h.tensor_tensor(out=u_[:], in0=cc[2][tq % 2][:], in1=cc[1][tq % 2][:], op=ALU.mult),
                          r=[cc[2][tq % 2], cc[1][tq % 2]], w=[u_])
                    kb.op("pool", lambda h: h.tensor_copy(out=ub[:], in_=u_[:]), r=[u_], w=[ub])
                    rows = slice(j * 128, (j + 1) * 128)
                    cols = slice(tq * 512, (tq + 1) * 512)
                    kb.dma("sp", self.u32_v[rows, cols], u_[:], r=[u_])
                    kb.dma("sp", self.ubf_v[rows, cols], ub[:], r=[ub])
                    kb.dma("sp", self.x0_v[rows, cols], cc[0][tq % 2][:], r=[cc[0][tq % 2]])

                for tt in range(16):
                    for pt in range(3):
                        ps = pz[pt][tt % 2]
                        for kc in range(8):
                            kb.op("pe", lambda h, kc=kc: h.matmul(ps[:], lhsT=wj_[:, kc, pt * 128:(pt + 1) * 128],
                                                                  rhs=hT[:, kc, tt * 512:(tt + 1) * 512],
                                                                  start=(kc == 0), stop=(kc == 7)), r=[wj_, hk[tt]], w=[ps])
                        z = Z[pt][tt % 3]
                        idx = pt * 8 + j
                        kb.op("act", lambda h: h.activation(out=z[:, 1:513], in_=ps[:], func=AF.Identity,
                                                            bias=vb[:, 0, idx:idx + 1], scale=1.0), r=[ps, vb], w=[z])
                        if tt == 0:
                            kb.op("pool", lambda h: h.memset(z[:, 0:1], 0.0), w=[z])
                        else:
                            zp = Z[pt][(tt - 1) % 3]
                            kb.op("pool", lambda h: h.tensor_copy(out=z[:, 0:1], in_=zp[:, 512:513]), r=[zp], w=[z])
                            kb.op("pool", lambda h: h.tensor_copy(out=zp[:, 513:514], in_=z[:, 1:2]), r=[z], w=[zp])
                        if tt == 15:
                            kb.op("pool", lambda h: h.memset(z[:, 513:514], 0.0), w=[z])
                    if tt >= 1:
                        conv(tt - 1)
                conv(15)

    def ph_hy_fft(self):
        kb = self.kb
        nc = self.nc
        PI = math.pi
        hd_d = self.scratch("hd_d", [128, NFFT], BF16)
        with kb.phase() as st:
            w1d = kb.sbuf(st, "w1d", [33, 128], F32)
            w2bd = kb.sbuf(st, "w2bd", [128, 128], F32)
            fv = kb.sbuf(st, "fv", [128, 6], F32)
            kb.op("dve", lambda h: h.memset(w2bd[:], 0.0), w=[w2bd])
            with nc.allow_non_contiguous_dma(reason="tiny loads"):
                for hlf in range(2):
                    rs_ = slice(hlf * 64, (hlf + 1) * 64)
                    kb.dma("sp", w1d[:, rs_], self.hy_f_w1.ap()[0][:, :], w=[w1d], nowaw=True)
                    kb.dma("sp", w2bd[rs_, rs_], self.hy_f_w2.ap()[0][:, :], w=[w2bd], nowaw=(hlf > 0))
                    for i, src in enumerate((self.hy_f_b1, self.hy_f_freq1, self.hy_f_b2, self.hy_f_freq2)):
                        kb.dma("sp", fv[rs_, i:i + 1], src.ap()[0].rearrange("(p o) -> p o", o=1), w=[fv], nowaw=True)
            kb.op("dve", lambda h: h.tensor_tensor(out=fv[:, 4:5], in0=fv[:, 0:1], in1=fv[:, 1:2], op=ALU.mult), r=[fv], w=[fv])
            kb.op("dve", lambda h: h.tensor_tensor(out=fv[:, 5:6], in0=fv[:, 2:3], in1=fv[:, 3:4], op=ALU.mult), r=[fv], w=[fv])
            kb.op("dve", lambda h: h.tensor_scalar(out=fv[:], in0=fv[:], scalar1=1.0 / (2.0 * PI), scalar2=None, op0=ALU.mult), r=[fv], w=[fv])
            zt = [kb.sbuf(st, "zt%d" % i, [33, 512], F32) for i in range(2)]
            a1 = [kb.sbuf(st, "a1_%d" % i, [128, 512], F32) for i in range(2)]
            h1 = [kb.sbuf(st, "h1_%d" % i, [128, 512], F32) for i in range(2)]
            a2 = [kb.sbuf(st, "a2_%d" % i, [128, 512], F32) for i in range(2)]
            hdc = [kb.sbuf(st, "hdc%d" % i, [128, 512], BF16) for i in range(2)]
            p1 = [kb.psum(st, "p1_%d" % i, [128, 512], F32) for i in range(2)]
            p2 = [kb.psum(st, "p2_%d" % i, [128, 512], F32) for i in range(2)]
            for pc in range(32):
                i = pc % 2
                cols = slice(pc * 512, (pc + 1) * 512)
                kb.dma("sp", zt[i][:], self.hy_zt.ap()[:, cols], w=[zt[i]])
                kb.op("pe", lambda h: h.matmul(p1[i][:], lhsT=w1d[:], rhs=zt[i][:], start=True, stop=True), r=[w1d, zt[i]], w=[p1[i]])
                kb.op("act", lambda h: h.activation(out=a1[i][:], in_=p1[i][:], func=AF.Identity, scale=fv[:, 1:2], bias=fv[:, 4:5]),
                      r=[p1[i], fv], w=[a1[i]])
                for _rep in range(2):
                    kb.op("dve", lambda h: h.scalar_tensor_tensor(out=a1[i][:], in0=a1[i][:], scalar=0.5, in1=a1[i][:],
                                                                  op0=ALU.is_gt, op1=ALU.subtract), r=[a1[i]], w=[a1[i]])
                kb.op("act", lambda h: h.activation(out=h1[i][:], in_=a1[i][:], func=AF.Sin, scale=2.0 * PI), r=[a1[i]], w=[h1[i]])
                kb.op("pe", lambda h: h.matmul(p2[i][:], lhsT=w2bd[:], rhs=h1[i][:], start=True, stop=True), r=[w2bd, h1[i]], w=[p2[i]])
                kb.op("act", lambda h: h.activation(out=a2[i][:], in_=p2[i][:], func=AF.Identity, scale=fv[:, 3:4], bias=fv[:, 5:6]),
                      r=[p2[i], fv], w=[a2[i]])
                for _rep in range(2):
                    kb.op("dve", lambda h: h.scalar_tensor_tensor(out=a2[i][:], in0=a2[i][:], scalar=0.5, in1=a2[i][:],
                                                                  op0=ALU.is_gt, op1=ALU.subtract), r=[a2[i]], w=[a2[i]])
                kb.op("act", lambda h: h.activation(out=hdc[i][:], in_=a2[i][:], func=AF.Sin, scale=2.0 * PI), r=[a2[i]], w=[hdc[i]])
                if pc < 16:
                    kb.op("pool", lambda h: h.memset(hdc[i][64:128, :], 0.0), w=[hdc[i]])
                else:
                    kb.op("pool", lambda h: h.memset(hdc[i][0:64, :], 0.0), w=[hdc[i]])
                    if pc == 16:
                        kb.op("pool", lambda h: h.memset(hdc[i][64:128, 0:1], 0.0), w=[hdc[i]])
                kb.dma("sp", hd_d.ap()[:, cols], hdc[i][:], r=[hdc[i]])
            if "dbg_hd" in self.taps:
                pass
        with kb.phase() as st:
            f1 = kb.sbuf(st, "f1", [128, 128], BF16)
            r1 = kb.sbuf(st, "r1", [128, 256], BF16)
            r2 = kb.sbuf(st, "r2", [128, 256], BF16)
            tpos = kb.sbuf(st, "tpos", [128, 128], F32)
            negdec = kb.sbuf(st, "negdec", [128, D], F32)
            woutS = kb.sbuf(st, "woutS", [128, D], BF16)
            skipT = kb.sbuf(st, "skipT", [128, 8], F32)
            onesf = kb.sbuf(st, "onesf", [128, 1], F32)
            kb.dma("sp", f1[:], self.fft_f1.ap()[:, :], w=[f1])
            kb.dma("sp", r1[:], self.fft_r1.ap()[:, :], w=[r1])
            kb.dma("sp", r2[:], self.fft_r2.ap()[:, :], w=[r2])
            kb.dma("sp", tpos[:], self.hy_tpos.ap()[:, :], w=[tpos])
            kb.dma("sp", negdec[:], self.hy_decay.ap()[0].partition_broadcast(128), w=[negdec])
            kb.op("act", lambda h: h.activation(out=negdec[:], in_=negdec[:], func=AF.Abs), r=[negdec], w=[negdec])
            kb.op("act", lambda h: h.mul(out=negdec[:], in_=negdec[:], mul=-1.0), r=[negdec], w=[negdec])
            kb.dma("pool", woutS[0:64, :], self.hy_f_wout.ap()[0][:, 0:D], w=[woutS])
            kb.dma("pool", woutS[64:128, :], self.hy_f_wout.ap()[0][:, D:2 * D], w=[woutS], nowaw=True)
            kb.op("act", lambda h: h.mul(out=woutS[64:128, :], in_=woutS[64:128, :], mul=-1.0), r=[woutS], w=[woutS])
            with nc.allow_non_contiguous_dma(reason="tiny loads"):
                kb.dma("sp", skipT[:], self.hy_skip.ap()[0].rearrange("(c p) -> p c", p=128), w=[skipT])
            kb.op("dve", lambda h: h.memset(onesf[:], 1.0), w=[onesf])

            RA = kb.sbuf(st, "RA", [128, 32768], BF16)
            RB = kb.sbuf(st, "RB", [128, 8192], F32)
            RC = kb.sbuf(st, "RC", [128, 16384], BF16)
            kU = kb.key("kU")
            kAT = kb.key("kAT")
            Uv = RA[:, 0:16384].rearrange("p (d b) -> p d b", b=128)
            ATv = RA[:, 16384:32768].rearrange("p (r k d) -> p r k d", r=2, k=64)
            Ev = RA[0:64, :].rearrange("p (r b d) -> p r b d", r=2, b=128)
            Kfv = RB[:].bitcast(BF16).rearrange("p (k r d) -> p k r d", k=64, r=2)
            yT = RB
            YTv = RC[:].rearrange("p (r d k) -> p r d k", r=2, d=128)
            Hdv = RC[:].rearrange("p (a b) -> p b a", b=128)
            PB = [kb.psum(st, "pb%d" % i, [128, 512], F32) for i in range(8)]
            self._pbi = 0

            def bank():
                self._pbi = (self._pbi + 1) % 8
                return PB[self._pbi]

            h2r = [kb.sbuf(st, "h2r%d" % i, [128, 3, 128], BF16) for i in range(6)]
            h4r = [kb.sbuf(st, "h4r%d" % i, [64, 16, 2, 64], BF16) for i in range(2)]
            warg = [kb.sbuf(st, "warg%d" % i, [128, 512], F32) for i in range(2)]
            win = [kb.sbuf(st, "win%d" % i, [128, 512], F32) for i in range(2)]
            ksum = kb.sbuf(st, "ksum", [128, 128], F32)
            rnorm = kb.sbuf(st, "rnorm", [128, 1], F32)
            tA = [kb.sbuf(st, "tA%d" % i, [128, 512], F32) for i in range(2)]
            tB = [kb.sbuf(st, "tB%d" % i, [128, 512], F32) for i in range(2)]
            uc = [kb.sbuf(st, "uc%d" % i, [128, 1024], F32) for i in range(2)]
            xc = [kb.sbuf(st, "xc%d" % i, [128, 1024], F32) for i in range(2)]
            vo = [kb.sbuf(st, "vo%d" % i, [128, 1024], BF16) for i in range(2)]
            self._ev = 0

            def evac(out_ap, in_ap, r, w):
                self._ev += 1
                if self._ev % 2 == 0:
                    kb.op("act", lambda h: h.copy(out=out_ap, in_=in_ap), r=r, w=w)
                else:
                    kb.op("dve", lambda h: h.tensor_copy(out=out_ap, in_=in_ap), r=r, w=w)

            h2cnt = [0]

            def s1(src_v, kparts, src_key):
                for dg in range(32):
                    ps = bank()
                    for dd in range(4):
                        d = dg * 4 + dd
                        kb.op("pe", lambda h: h.matmul(ps[:, dd * 128:(dd + 1) * 128], lhsT=src_v[0:kparts, d, :],
                                                       rhs=f1[0:kparts, :], start=True, stop=True), r=[src_key, f1], w=[ps])
                    evac(ATv[:, :, :, dg * 4:dg * 4 + 4], ps[:].rearrange("p (dd r k) -> p r k dd", dd=4, r=2), [ps], [kAT])

            def s2(consume):
                for kp in range(32):
                    ps = bank()
                    for kk in range(2):
                        k1 = 2 * kp + kk
                        h2 = h2r[h2cnt[0] % 6]
                        h2cnt[0] += 1
                        kb.dma("sp", h2[:], self.fft_h2.ap()[k1], w=[h2])
                        o_r = ps[:, (kk * 2) * 128:(kk * 2 + 1) * 128]
                        o_i = ps[:, (kk * 2 + 1) * 128:(kk * 2 + 2) * 128]
                        kb.op("pe", lambda h: h.matmul(o_r, lhsT=h2[:, 0, :], rhs=ATv[:, 0, k1, :], start=True, stop=False), r=[h2, kAT], w=[ps])
                        kb.op("pe", lambda h: h.matmul(o_r, lhsT=h2[:, 2, :], rhs=ATv[:, 1, k1, :], start=False, stop=True), r=[h2, kAT], w=[ps])
                        kb.op("pe", lambda h: h.matmul(o_i, lhsT=h2[:, 1, :], rhs=ATv[:, 0, k1, :], start=True, stop=False), r=[h2, kAT], w=[ps])
                        kb.op("pe", lambda h: h.matmul(o_i, lhsT=h2[:, 0, :], rhs=ATv[:, 1, k1, :], start=False, stop=True), r=[h2, kAT], w=[ps])
                    consume(kp, ps)

            for db in range(8):
                c0 = db * 128
                for q in range(4):
                    kb.dma("sp", RC[:, q * 4096:(q + 1) * 4096], hd_d.ap()[:, q * 4096:(q + 1) * 4096], w=[RC], nowaw=(q > 0))
                for bg in range(32):
                    ps = bank()
                    b0 = bg * 4
                    for bb in range(4):
                        kb.op("pe", lambda h: h.matmul(ps[:, bb * 128:(bb + 1) * 128], lhsT=Hdv[:, b0 + bb, :],
                                                       rhs=woutS[:, c0:c0 + 128], start=True, stop=True), r=[RC, woutS], w=[ps])
                    wa = warg[bg % 2]
                    wi = win[bg % 2]
                    kb.op("pool", lambda h: h.tensor_tensor(out=wa[:].rearrange("p (b d) -> p b d", d=128),
                                                           in0=tpos[:, b0:b0 + 4].unsqueeze(2).to_broadcast([128, 4, 128]),
                                                           in1=negdec[:, c0:c0 + 128].unsqueeze(1).to_broadcast([128, 4, 128]),
                                                           op=ALU.mult), r=[tpos, negdec], w=[wa])
                    kb.op("act", lambda h: h.activation(out=wi[:], in_=wa[:], func=AF.Exp), r=[wa], w=[wi])
                    kb.op("dve", lambda h: h.tensor_tensor(out=Uv[:, :, b0:b0 + 4].transpose([0, 2, 1]),
                                                           in0=ps[:].rearrange("p (b d) -> p b d", d=128),
                                                           in1=wi[:].rearrange("p (b d) -> p b d", d=128), op=ALU.mult),
                          r=[ps, wi], w=[kU])
                kb.op("dve", lambda h: h.tensor_reduce(out=ksum[:], in_=Uv, axis=AX.X, op=ALU.add, apply_absolute_value=True),
                      r=[kU], w=[ksum])
                pn = bank()
                kb.op("pe", lambda h: h.matmul(pn[:, 0:1], lhsT=ksum[:], rhs=onesf[:], start=True, stop=True), r=[ksum, onesf], w=[pn])
                kb.op("dve", lambda h: h.reciprocal(out=rnorm[:], in_=pn[:, 0:1]), r=[pn], w=[rnorm])
                s1(Uv, 128, kU)
                s2(lambda kp, ps: evac(Kfv[:, 2 * kp:2 * kp + 2, :, :], ps[:].rearrange("p (k r d) -> p k r d", k=2, r=2), [ps], [RB]))
                for q in range(4):
                    kb.dma("sp", Uv[0:64, q * 32:(q + 1) * 32, :],
                           self.ubf_v[c0 + q * 32:c0 + (q + 1) * 32, :].rearrange("d (a b) -> a d b", b=128),
                           w=[kU], nowaw=(q > 0))
                s1(Uv, 64, kU)

                def product(kp, ps):
                    X = ps[:].rearrange("p (k r d) -> p k r d", k=2, r=2)
                    Kk = Kfv[:, 2 * kp:2 * kp + 2, :, :]
                    ta = tA[kp % 2]
                    tb = tB[kp % 2]
                    ta4 = ta[:].rearrange("p (k r d) -> p k r d", k=2, r=2)
                    tb4 = tb[:].rearrange("p (k r d) -> p k r d", k=2, r=2)
                    kb.op("dve", lambda h: h.tensor_tensor(out=ta4, in0=X, in1=Kk, op=ALU.mult), r=[ps, RB], w=[ta])
                    kb.op("dve", lambda h: h.tensor_tensor(out=tb4[:, :, 0, :], in0=X[:, :, 0, :], in1=Kk[:, :, 1, :], op=ALU.mult),
                          r=[ps, RB], w=[tb])
                    kb.op("dve", lambda h: h.tensor_tensor(out=tb4[:, :, 1, :], in0=X[:, :, 1, :], in1=Kk[:, :, 0, :], op=ALU.mult),
                          r=[ps, RB], w=[tb])
                    kb.op("pool", lambda h: h.tensor_tensor(out=YTv[:, 0, :, 2 * kp:2 * kp + 2].transpose([0, 2, 1]),
                                                            in0=ta4[:, :, 0, :], in1=ta4[:, :, 1, :], op=ALU.subtract),
                          r=[ta], w=[RC])
                    kb.op("pool", lambda h: h.tensor_tensor(out=YTv[:, 1, :, 2 * kp:2 * kp + 2].transpose([0, 2, 1]),
                                                            in0=tb4[:, :, 0, :], in1=tb4[:, :, 1, :], op=ALU.add),
                          r=[tb], w=[RC])

                s2(product)
                for dg in range(64):
                    ps = bank()
                    for dd in range(2):
                        d = dg * 2 + dd
                        o_ = ps[0:64, dd * 256:(dd + 1) * 256]
                        kb.op("pe", lambda h: h.matmul(o_, lhsT=YTv[:, 0, d, :], rhs=r1[:], start=True, stop=False), r=[RC, r1], w=[ps])
                        kb.op("pe", lambda h: h.matmul(o_, lhsT=YTv[:, 1, d, :], rhs=r2[:], start=False, stop=True), r=[RC, r2], w=[ps])
                    evac(Ev[:, :, :, dg * 2:dg * 2 + 2], ps[0:64, :].rearrange("p (dd r b) -> p r b dd", dd=2, r=2), [ps], [kU, kAT])
                for bg in range(16):
                    if bg % 2 == 0:
                        h4 = h4r[(bg // 2) % 2]
                        kb.dma("sp", h4[:], self.fft_h4.ap()[:, bg * 8:bg * 8 + 16, :, :], w=[h4])
                    ps = bank()
                    for bb in range(8):
                        b = bg * 8 + bb
                        o_ = ps[:, bb * 64:(bb + 1) * 64]
                        kb.op("pe", lambda h: h.matmul(o_, lhsT=Ev[:, 0, b, :], rhs=h4[:, b % 16, 0, :], start=True, stop=False),
                              r=[kU, kAT, h4], w=[ps])
                        kb.op("pe", lambda h: h.matmul(o_, lhsT=Ev[:, 1, b, :], rhs=h4[:, b % 16, 1, :], start=False, stop=True),
                              r=[kU, kAT, h4], w=[ps])
                    evac(yT[:].rearrange("p (a b) -> p a b", b=128)[:, :, bg * 8:bg * 8 + 8],
                         ps[:].rearrange("p (bb a) -> p a bb", bb=8), [ps], [RB])
                for ck in range(8):
                    cols = slice(ck * 1024, (ck + 1) * 1024)
                    u_ = uc[ck % 2]
                    x_ = xc[ck % 2]
                    v_ = vo[ck % 2]
                    kb.dma("sp", u_[:], self.u32_v[c0:c0 + 128, cols], w=[u_])
                    kb.dma("sp", x_[:], self.x0_v[c0:c0 + 128, cols], w=[x_])
                    kb.op("act", lambda h: h.activation(out=u_[:], in_=u_[:], func=AF.Identity, scale=skipT[:, db:db + 1], bias=0.0),
                          r=[u_, skipT], w=[u_])
                    kb.op("dve", lambda h: h.scalar_tensor_tensor(out=u_[:], in0=yT[:, cols], scalar=rnorm[:, 0:1], in1=u_[:],
                                                                  op0=ALU.mult, op1=ALU.add), r=[RB, rnorm, u_], w=[u_])
                    kb.op("pool", lambda h: h.tensor_tensor(out=v_[:], in0=u_[:], in1=x_[:], op=ALU.mult), r=[u_, x_], w=[v_])
                    kb.dma("sp", self.vT_v[c0:c0 + 128, cols], v_[:], r=[v_])

    def ph_hy_out(self, xin, xout):
        kb = self.kb
        with kb.phase() as st:
            wob = kb.sbuf(st, "wob", [128, 8, D], BF16)
            kb.dma("pool", wob[:], self.hy_w_out.ap()[0].rearrange("(kc p) n -> p kc n", p=128), w=[wob])
            boutb = kb.sbuf(st, "boutb", [128, D], F32)
            kb.dma("sp", boutb[:], self.hy_b_out.ap()[0].partition_broadcast(128), w=[boutb])
            g1b = kb.sbuf(st, "g1b", [128, D], F32)
            kb.dma("sp", g1b[:], self.gb_d[1][0].ap()[:, :], w=[g1b])
            vt = [kb.sbuf(st, "vt%d" % i, [128, 8, 512], BF16) for i in range(2)]
            xt = [kb.sbuf(st, "xt%d" % i, [128, D], F32) for i in range(4)]
            ot = [kb.sbuf(st, "ot%d" % i, [128, 512], F32) for i in range(3)]
            po = [kb.psum(st, "po%d" % i, [128, 512], F32) for i in range(4)]
            xin_t = xin.rearrange("(n p) d -> n p d", p=128)
            xout_t = xout.rearrange("(n p) d -> n p d", p=128)
            vsrc = self.vT_v.rearrange("(kc p) t -> p kc t", p=128)
            for tt in range(16):
                v_ = vt[tt % 2]
                kb.dma("sp", v_[:], vsrc[:, :, tt * 512:(tt + 1) * 512], w=[v_])
                for s in range(4):
                    n = tt * 4 + s
                    b = xt[n % 4]
                    kb.dma("sp", b[:], xin_t[n], w=[b])
                    for hf in range(2):
                        cs = slice(hf * 512, (hf + 1) * 512)
                        p = po[(n * 2 + hf) % 4]
                        for kc in range(8):
                            kb.op("pe", lambda h, kc=kc: h.matmul(p[:], lhsT=v_[:, kc, s * 128:(s + 1) * 128], rhs=wob[:, kc, cs],
                                                                  start=(kc == 0), stop=(kc == 7)), r=[v_, wob], w=[p])
                        o_ = ot[(n * 2 + hf) % 3]
                        kb.op("dve", lambda h: h.tensor_tensor(out=o_[:], in0=p[:], in1=boutb[:, cs], op=ALU.add), r=[p, boutb], w=[o_])
                        kb.op("dve", lambda h: h.tensor_tensor(out=o_[:], in0=o_[:], in1=g1b[:, cs], op=ALU.mult), r=[o_, g1b], w=[o_])
                        kb.op("pool", lambda h: h.tensor_tensor(out=o_[:], in0=o_[:], in1=b[:, cs], op=ALU.add), r=[o_, b], w=[o_])
                        kb.dma("sp", xout_t[n][:, cs], o_[:], r=[o_])

    def norm_T_a1(self, xt, xkeys, ws):
        kb = self.kb
        ss, rs, rstd, junk, xs = ws["ss"], ws["rs"], ws["rstd"], ws["junk"], ws["xs"]
        kb.op("act", lambda h: h.activation(out=junk[:], in_=xt, func=AF.Square, accum_out=ss[:, 0:1]),
              r=list(xkeys), w=[junk, ss])
        kb.op("act", lambda h: h.activation(out=rs[:, 0:1], in_=ss[:, 0:1], func=AF.Sqrt, bias=EPS, scale=1.0 / D),
              r=[ss], w=[rs])
        kb.op("dve", lambda h: h.reciprocal(out=rstd[:, 0:1], in_=rs[:, 0:1]), r=[rs], w=[rstd])
        kb.op("dve", lambda h: h.tensor_scalar(out=xs[:], in0=xt, scalar1=rstd[:, 0:1], scalar2=None, op0=ALU.mult),
              r=[rstd] + list(xkeys), w=[xs])

    def norm_T_a2(self, ws):
        kb = self.kb
        xs = ws["xs"]
        pT, pTk = ws["pT_ap"], ws["pT_key"]
        for c in range(8):
            kb.op("pe", lambda h, c=c: h.transpose(pT[:, c, :], xs[:, c * 128:(c + 1) * 128], self.idb[:]),
                  r=[xs, self.idb], w=[pTk])

    def norm_T_b(self, hT_ap_fn, hkey, A, S, AS_keys, ws):
        kb = self.kb
        pT, pTk = ws["pT_ap"], ws["pT_key"]
        for c in range(8):
            kb.op("act", lambda h, c=c: h.activation(out=hT_ap_fn(c), in_=pT[:, c, :], func=AF.Identity,
                                                     scale=A[:, c:c + 1], bias=S[:, c:c + 1]),
                  r=[pTk] + list(AS_keys), w=[hkey])

    def norm_T(self, xt, xkeys, hT_ap_fn, hkey, A, S, AS_keys, ws):
        self.norm_T_a1(xt, xkeys, ws)
        self.norm_T_a2(ws)
        self.norm_T_b(hT_ap_fn, hkey, A, S, AS_keys, ws)

    def ph_ffn(self, li, xin, xout):
        kb = self.kb
        nc = self.nc
        TT = 256
        NS = TT // 128
        with kb.phase() as st:
            w1b = kb.sbuf(st, "w1b", [128, 8, DFF], BF16)
            w3b = kb.sbuf(st, "w3b", [128, 8, DFF], BF16)
            w2b = kb.sbuf(st, "w2b", [128, NJ, D], BF16)
            kb.dma("pool", w1b[:], self.ffn_w1[li].ap().rearrange("(kc p) n -> p kc n", p=128), w=[w1b])
            kb.dma("pool", w3b[:], self.ffn_w3[li].ap().rearrange("(kc p) n -> p kc n", p=128), w=[w3b])
            kb.dma("pool", w2b[:], self.ffn_w2[li].ap().rearrange("(j p) n -> p j n", p=128), w=[w2b])
            NXB = 4
            xt = [kb.sbuf(st, "xt%d" % i, [128, D], F32) for i in range(NXB)]
            hT = [kb.sbuf(st, "hT%d" % i, [128, 8, TT], BF16) for i in range(2)]
            act = kb.sbuf(st, "act", [128, NJ, TT], BF16)
            ws = []
            for i in range(2):
                ws.append(dict(ss=kb.sbuf(st, "ss%d" % i, [128, 1], F32), rs=kb.sbuf(st, "rs%d" % i, [128, 1], F32),
                               rstd=kb.sbuf(st, "rstd%d" % i, [128, 1], F32),
                               junk=None,
                               xs=kb.sbuf(st, "xs%d" % i, [128, D], BF16),
                               pT_key=kb.psum(st, "pT%d" % i, [128, 8, 128], BF16)))
                ws[-1]["pT_ap"] = ws[-1]["pT_key"][:]
            sil = [kb.sbuf(st, "sil%d" % i, [128, TT], F32) for i in range(2)]
            ot = [kb.sbuf(st, "ot%d" % i, [128, 512], F32) for i in range(3)]
            junk1 = kb.sbuf(st, "junk", [128, D], BF16)
            for w_ in ws:
                w_["junk"] = junk1
            pa = [kb.psum(st, "pa%d" % i, [128, 2, TT], F32) for i in range(2)]
            po = [kb.psum(st, "po%d" % i, [128, 512], F32) for i in range(4)]
            A2 = self.A[li][1]
            S2 = self.mT[li]
            g2b = kb.sbuf(st, "g2b", [128, D], F32)
            kb.dma("sp", g2b[:], self.gb_d[li][1].ap()[:, :], w=[g2b])
            xin_t = xin.rearrange("(n p) d -> n p d", p=128)
            xout_t = xout.rearrange("(n p) d -> n p d", p=128)
            ntt = L // TT

            def load(tt):
                for s in range(NS):
                    n = tt * NS + s
                    b = xt[n % NXB]
                    kb.dma("sp", b[:], xin_t[n], w=[b])

            def norm_part(tt, part):
                hh = hT[tt % 2]
                for s in range(NS):
                    n = tt * NS + s
                    b = xt[n % NXB]
                    w_ = ws[n % 2]
                    if part == 0:
                        self.norm_T_a1(b[:], [b], w_)
                    elif part == 1:
                        self.norm_T_a2(w_)
                    else:
                        self.norm_T_b(lambda c, s=s: hh[:, c, s * 128:(s + 1) * 128], hh, A2, S2[:, 24:32], [A2, S2], w_)

            load(0)
            for part in range(3):
                norm_part(0, part)
            for tt in range(ntt):
                if tt + 1 < ntt:
                    load(tt + 1)
                h_ = hT[tt % 2]
                for j in range(NJ):
                    if tt + 1 < ntt and j in (3, 11, 13):
                        norm_part(tt + 1, {3: 0, 11: 1, 13: 2}[j])
                    p = pa[j % 2]
                    for kc in range(8):
                        kb.op("pe", lambda h, kc=kc: h.matmul(p[:, 0, :], lhsT=w1b[:, kc, j * 128:(j + 1) * 128],
                                                              rhs=h_[:, kc, :], start=(kc == 0), stop=(kc == 7)),
                              r=[w1b, h_], w=[p])
                    for kc in range(8):
                        kb.op("pe", lambda h, kc=kc: h.matmul(p[:, 1, :], lhsT=w3b[:, kc, j * 128:(j + 1) * 128],
                                                              rhs=h_[:, kc, :], start=(kc == 0), stop=(kc == 7)),
                              r=[w3b, h_], w=[p])
                    sl = sil[j % 2]
                    kb.op("act", lambda h: h.activation(out=sl[:], in_=p[:, 0, :], func=AF.Silu), r=[p], w=[sl])
                    kb.op("dve", lambda h: h.tensor_tensor(out=act[:, j, :], in0=p[:, 1, :], in1=sl[:], op=ALU.mult),
                          r=[p, sl], w=[act])
                for s in range(NS):
                    n = tt * NS + s
                    b = xt[n % NXB]
                    for hf in range(2):
                        o_ = ot[(n * 2 + hf) % 3]
                        p = po[(s * 2 + hf) % 4]
                        for j in range(NJ):
                            kb.op("pe", lambda h, j=j: h.matmul(p[:], lhsT=act[:, j, s * 128:(s + 1) * 128],
                                                                rhs=w2b[:, j, hf * 512:(hf + 1) * 512],
                                                                start=(j == 0), stop=(j == NJ - 1)),
                                  r=[act, w2b], w=[p])
                        cs = slice(hf * 512, (hf + 1) * 512)
                        kb.op("dve", lambda h: h.tensor_tensor(out=o_[:], in0=p[:], in1=g2b[:, cs], op=ALU.mult),
                              r=[p, g2b], w=[o_])
                        kb.op("pool", lambda h: h.tensor_tensor(out=o_[:], in0=o_[:], in1=b[:, cs], op=ALU.add),
                              r=[o_, b], w=[o_])
                        kb.dma("sp", xout_t[n][:, cs], o_[:], r=[o_])


def build_program(stop_after=None, taps=(), start_at=None, mode="full"):
    p = Prog(stop_after=stop_after, taps=taps, start_at=start_at, mode=mode)
    nc = p.build()
    return p, nc


_STACKED = ("ada_w", "ada_b", "norm1_g", "norm2_g", "ffn_w1_", "ffn_w3_", "ffn_w2_")


def make_in_maps(inputs, p, xs=None):
    hc = host_consts()
    f = lambda a: np.ascontiguousarray(np.asarray(a, dtype=np.float32))
    shared = {}
    for name in p.din:
        if name in ("x", "c", "ctx"):
            continue
        if name in hc:
            shared[name] = hc[name]
            continue
        for base in _STACKED:
            if name.startswith(base) and name[len(base):].isdigit():
                shared[name] = f(inputs[base.rstrip("_")][int(name[len(base):])])
                break
        else:
            shared[name] = f(inputs[name])
    maps = []
    for b in range(NCORES):
        m = dict(shared)
        m["x"] = f(inputs["x"][b]) if xs is None else xs[b]
        m["c"] = f(inputs["c"][b])
        if "ctx" in p.din:
            m["ctx"] = f(inputs["ctx"][b])
        maps.append(m)
    return maps


SPLIT = False


def kernel(**inputs):
    if not SPLIT:
        p, nc = build_program()
        res = run_bass_kernel_spmd(nc, make_in_maps(inputs, p), core_ids=list(range(NCORES)))
        out = np.stack([np.asarray(r["out"]) for r in res.results], axis=0)
        return out.astype(np.float32)
    p0, nc0 = build_program(mode="l0")
    res0 = run_bass_kernel_spmd(nc0, make_in_maps(inputs, p0), core_ids=list(range(NCORES)))
    x1 = [np.ascontiguousarray(np.asarray(r["out"], dtype=np.float32)) for r in res0.results]
    p1, nc1 = build_program(mode="l1")
    res1 = run_bass_kernel_spmd(nc1, make_in_maps(inputs, p1, xs=x1), core_ids=list(range(NCORES)))
    out = np.stack([np.asarray(r["out"]) for r in res1.results], axis=0)
    return out.astype(np.float32)
```
